# Optimizing a Trainium2 kernel written in Bass

```python
import math
import jax, jax.numpy as jnp
from jax import lax
import numpy as np

D_MODEL = 1024
BATCH = 16
SEQ = 2048
DEPTH = 1
DEC_BATCH = 32
DEC_SEQ = 16
PAST_LEN = 2048

CHUNK = 64
Q_BLOCK = 128
BRANCH_WIDTH = D_MODEL // 2
DA_HEAD_DIM = 64
DA_HEADS = BRANCH_WIDTH // (2 * DA_HEAD_DIM)
DA_V = 2 * DA_HEAD_DIM
SB_HEAD_DIM = 64
SB_HEADS = BRANCH_WIDTH // SB_HEAD_DIM
D_FF = 2816
CONV_WIDTH = 3
ROPE_THETA = 10000.0
NORM_EPS = 1e-6
N_IN = 6 * BRANCH_WIDTH + 2 * D_MODEL
NEG_INF = -1e30

kernel_name = "gated_diffattn_stickbreak_convffn_stream_step"


def rms_norm(x, g):
    xf = x.astype(jnp.float32)
    y = xf * lax.rsqrt(jnp.mean(xf * xf, axis=-1, keepdims=True) + NORM_EPS)
    return (y * g.astype(jnp.float32)).astype(x.dtype)


def rope(x, pos):
    dh = x.shape[-1]
    half = dh // 2
    inv = ROPE_THETA ** (-jnp.arange(half, dtype=jnp.float32) * 2.0 / dh)
    ang = pos.astype(jnp.float32)[:, None] * inv[None, :]
    cos = jnp.cos(ang)[:, None, None, :]
    sin = jnp.sin(ang)[:, None, None, :]
    xf = x.astype(jnp.float32)
    x1, x2 = xf[..., :half], xf[..., half:]
    out = jnp.concatenate([x1 * cos - x2 * sin, x2 * cos + x1 * sin], axis=-1)
    return out.astype(x.dtype)


def split_projection(z, pos):
    lead = z.shape[:2]
    W = BRANCH_WIDTH
    qa = rope(z[..., 0:W].reshape(lead + (DA_HEADS, 2, DA_HEAD_DIM)), pos)
    ka = rope(z[..., W:2 * W].reshape(lead + (DA_HEADS, 2, DA_HEAD_DIM)), pos)
    va = z[..., 2 * W:3 * W].reshape(lead + (DA_HEADS, DA_V))
    qb = z[..., 3 * W:4 * W].reshape(lead + (SB_HEADS, SB_HEAD_DIM))
    kb = z[..., 4 * W:5 * W].reshape(lead + (SB_HEADS, SB_HEAD_DIM))
    vb = z[..., 5 * W:6 * W].reshape(lead + (SB_HEADS, SB_HEAD_DIM))
    gate_a = jax.nn.sigmoid(z[..., 6 * W:6 * W + D_MODEL])
    gate_b = jax.nn.sigmoid(z[..., 6 * W + D_MODEL:])
    return qa, ka, va, qb, kb, vb, gate_a, gate_b


def diff_attention(q, k, v, q_pos, k_pos, lam):
    s = jnp.einsum('bqhcd,bkhcd->bhcqk', q, k).astype(jnp.float32) * (DA_HEAD_DIM ** -0.5)
    mask = (k_pos[None, :] // CHUNK) <= (q_pos[:, None] // CHUNK)
    p = jax.nn.softmax(jnp.where(mask, s, NEG_INF), axis=-1)
    w = p[:, :, 0] - lam * p[:, :, 1]
    return jnp.einsum('bhqk,bkhe->bqhe', w.astype(v.dtype), v)


def stick_breaking(q, k, v, q_pos, k_pos):
    z = jnp.einsum('bqhd,bkhd->bhqk', q, k).astype(jnp.float32) * (SB_HEAD_DIM ** -0.5)
    mask = k_pos[None, :] < q_pos[:, None]
    log_keep = jnp.where(mask, -jax.nn.softplus(z), 0.0)
    later = lax.cumsum(log_keep, axis=3, reverse=True) - log_keep
    a = jnp.where(mask, jnp.exp(jax.nn.log_sigmoid(z) + later), 0.0)
    return jnp.einsum('bhqk,bkhd->bqhd', a.astype(v.dtype), v)


def merge_branches(oa, ob, gate_a, gate_b, subln_g, lam_init, w_br_a, w_br_b, w_out):
    lead = oa.shape[:2]
    oa = rms_norm(oa, subln_g) * (1.0 - lam_init)
    br_a = oa.reshape(lead + (BRANCH_WIDTH,)) @ w_br_a
    br_b = ob.reshape(lead + (BRANCH_WIDTH,)) @ w_br_b
    return (gate_a * br_a + gate_b * br_b) @ w_out


def conv_ffn(x, prev, g, w_up, conv_w, conv_b, w_down):
    T = x.shape[1]
    u = rms_norm(x, g) @ w_up
    full = jnp.concatenate([prev.astype(u.dtype), u], axis=1)
    c = conv_b
    for j in range(CONV_WIDTH):
        c = c + conv_w[j] * full[:, j:j + T]
    gate, val = c[..., :D_FF], c[..., D_FF:]
    y = (jax.nn.silu(gate) * val) @ w_down
    return x + y, full[:, -(CONV_WIDTH - 1):]


def setup_inputs(seed: int = 0) -> dict:
    key = jax.random.key(seed)
    ks = jax.random.split(key, 32)
    f32 = jnp.float32
    nrm = lambda k, s, scale: jax.random.normal(k, s, f32) * scale
    return {
        "x_prompt": nrm(ks[0], (BATCH, SEQ, D_MODEL), 1.0),
        "x_sample": nrm(ks[1], (DEC_BATCH, DEC_SEQ, D_MODEL), 1.0),
        "cache_diff_k": nrm(ks[2], (DEPTH, DEC_BATCH, PAST_LEN, DA_HEADS, 2, DA_HEAD_DIM), 1.0),
        "cache_diff_v": nrm(ks[3], (DEPTH, DEC_BATCH, PAST_LEN, DA_HEADS, DA_V), 1.0),
        "cache_sb_k": nrm(ks[4], (DEPTH, DEC_BATCH, PAST_LEN, SB_HEADS, SB_HEAD_DIM), 1.0),
        "cache_sb_v": nrm(ks[5], (DEPTH, DEC_BATCH, PAST_LEN, SB_HEADS, SB_HEAD_DIM), 1.0),
        "state_conv": nrm(ks[6], (DEPTH, DEC_BATCH, CONV_WIDTH - 1, 2 * D_FF), 1.0),
        "attn_norm_g": 1.0 + nrm(ks[7], (DEPTH, D_MODEL), 0.02),
        "w_in": nrm(ks[8], (DEPTH, D_MODEL, N_IN), D_MODEL ** -0.5),
        "lambda_q1": nrm(ks[9], (DEPTH, DA_HEAD_DIM), 0.1),
        "lambda_k1": nrm(ks[10], (DEPTH, DA_HEAD_DIM), 0.1),
        "lambda_q2": nrm(ks[11], (DEPTH, DA_HEAD_DIM), 0.1),
        "lambda_k2": nrm(ks[12], (DEPTH, DA_HEAD_DIM), 0.1),
        "subln_g": 1.0 + nrm(ks[13], (DEPTH, DA_V), 0.02),
        "w_branch_a": nrm(ks[14], (DEPTH, BRANCH_WIDTH, D_MODEL), BRANCH_WIDTH ** -0.5),
        "w_branch_b": nrm(ks[15], (DEPTH, BRANCH_WIDTH, D_MODEL), BRANCH_WIDTH ** -0.5),
        "w_out": nrm(ks[16], (DEPTH, D_MODEL, D_MODEL), D_MODEL ** -0.5),
        "ffn_norm_g": 1.0 + nrm(ks[17], (DEPTH, D_MODEL), 0.02),
        "w_up": nrm(ks[18], (DEPTH, D_MODEL, 2 * D_FF), D_MODEL ** -0.5),
        "conv_w": nrm(ks[19], (DEPTH, CONV_WIDTH, 2 * D_FF), CONV_WIDTH ** -0.5),
        "conv_b": nrm(ks[20], (DEPTH, 2 * D_FF), 0.01),
        "w_down": nrm(ks[21], (DEPTH, D_FF, D_MODEL), D_FF ** -0.5),
        "final_norm_g": 1.0 + nrm(ks[22], (D_MODEL,), 0.02),
    }


def reference(x_prompt, x_sample, cache_diff_k, cache_diff_v, cache_sb_k, cache_sb_v,
              state_conv, attn_norm_g, w_in, lambda_q1, lambda_k1, lambda_q2, lambda_k2,
              subln_g, w_branch_a, w_branch_b, w_out, ffn_norm_g, w_up, conv_w, conv_b,
              w_down, final_norm_g):
    seq = x_prompt.shape[1]
    t_new = x_sample.shape[1]
    past = cache_diff_k.shape[2]
    pos_p = jnp.arange(seq)
    pos_s = past + jnp.arange(t_new)
    k_pos_s = jnp.arange(past + t_new)

    xp, xs = x_prompt, x_sample
    p_dk, p_dv, p_sk, p_sv, p_cv = [], [], [], [], []
    s_dk, s_dv, s_sk, s_sv, s_cv = [], [], [], [], []
    for l in range(DEPTH):
        lam_init = 0.8 - 0.6 * math.exp(-0.3 * l)
        lam = (jnp.exp(jnp.sum(lambda_q1[l].astype(jnp.float32) * lambda_k1[l].astype(jnp.float32)))
               - jnp.exp(jnp.sum(lambda_q2[l].astype(jnp.float32) * lambda_k2[l].astype(jnp.float32)))
               + lam_init)

        zp = rms_norm(xp, attn_norm_g[l]) @ w_in[l]
        qa, ka, va, qb, kb, vb, ga, gb = split_projection(zp, pos_p)
        oa_blocks, ob_blocks = [], []
        for i in range(seq // Q_BLOCK):
            s0, e0 = i * Q_BLOCK, (i + 1) * Q_BLOCK
            qpos = jnp.arange(s0, e0)
            kpos = jnp.arange(e0)
            oa_blocks.append(diff_attention(qa[:, s0:e0], ka[:, :e0], va[:, :e0], qpos, kpos, lam))
            ob_blocks.append(stick_breaking(qb[:, s0:e0], kb[:, :e0], vb[:, :e0], qpos, kpos))
        oa = jnp.concatenate(oa_blocks, axis=1)
        ob = jnp.concatenate(ob_blocks, axis=1)
        xp = xp + merge_branches(oa, ob, ga, gb, subln_g[l], lam_init,
                                 w_branch_a[l], w_branch_b[l], w_out[l])
        zeros_prev = jnp.zeros((xp.shape[0], CONV_WIDTH - 1, 2 * D_FF), xp.dtype)
        xp, conv_p = conv_ffn(xp, zeros_prev, ffn_norm_g[l], w_up[l], conv_w[l], conv_b[l], w_down[l])
        p_dk.append(ka); p_dv.append(va); p_sk.append(kb); p_sv.append(vb); p_cv.append(conv_p)

        zs = rms_norm(xs, attn_norm_g[l]) @ w_in[l]
        qa2, ka2, va2, qb2, kb2, vb2, ga2, gb2 = split_projection(zs, pos_s)
        ka_all = jnp.concatenate([cache_diff_k[l].astype(ka2.dtype), ka2], axis=1)
        va_all = jnp.concatenate([cache_diff_v[l].astype(va2.dtype), va2], axis=1)
        kb_all = jnp.concatenate([cache_sb_k[l].astype(kb2.dtype), kb2], axis=1)
        vb_all = jnp.concatenate([cache_sb_v[l].astype(vb2.dtype), vb2], axis=1)
        oa2 = diff_attention(qa2, ka_all, va_all, pos_s, k_pos_s, lam)
        ob2 = stick_breaking(qb2, kb_all, vb_all, pos_s, k_pos_s)
        xs = xs + merge_branches(oa2, ob2, ga2, gb2, subln_g[l], lam_init,
                                 w_branch_a[l], w_branch_b[l], w_out[l])
        xs, conv_s = conv_ffn(xs, state_conv[l], ffn_norm_g[l], w_up[l], conv_w[l], conv_b[l], w_down[l])
        s_dk.append(ka2); s_dv.append(va2); s_sk.append(kb2); s_sv.append(vb2); s_cv.append(conv_s)

    y_prompt = rms_norm(xp, final_norm_g)
    y_sample = rms_norm(xs, final_norm_g)
    return (y_prompt, y_sample,
            jnp.stack(p_dk), jnp.stack(p_dv), jnp.stack(p_sk), jnp.stack(p_sv), jnp.stack(p_cv),
            jnp.stack(s_dk), jnp.stack(s_dv), jnp.stack(s_sk), jnp.stack(s_sv), jnp.stack(s_cv))
```

```python
import numpy as np
import ml_dtypes
from contextlib import ExitStack
import concourse.bass as bass
import concourse.mybir as mybir
from concourse.bass_utils import run_bass_kernel_spmd

F32 = mybir.dt.float32
BF16 = mybir.dt.bfloat16
ALU = mybir.AluOpType
AF = mybir.ActivationFunctionType
AX = mybir.AxisListType

NCORES = 8
D = 1024
SEQ = 2048
NBLK = SEQ // 128
PB = 2
SB = 4
TS = 16
DFF = 2816
NCH = DFF // 128
NT = 4
TOK = NT * 128
NSLOT = 4
EPS = 1e-6
NEG = -30000.0


import types


def _snap(fn):
    if fn.__closure__ is None:
        return fn
    cells = []
    for c in fn.__closure__:
        try:
            cells.append(types.CellType(c.cell_contents))
        except ValueError:
            cells.append(c)
    return types.FunctionType(fn.__code__, fn.__globals__, fn.__name__, fn.__defaults__, tuple(cells))


class Sched:
    ENG = ("pe", "act", "dve", "pool", "sp")
    RAW_DIST = 10 ** 9

    def __init__(self, nc, same_engine_sync=True):
        self.nc = nc
        self.ops = []
        self.same = same_engine_sync
        self.last_w = {}
        self.readers = {}
        self.ecount = {}
        import os
        self.strict = os.environ.get("KSTRICT", "1") == "1"

    def add(self, eng, fn, reads=(), writes=(), dma=None):
        n = len(self.ops)
        deps = set()
        raw = set()
        for k in reads:
            if k in self.last_w:
                deps.add(self.last_w[k])
                raw.add(self.last_w[k])
            if isinstance(k, str) and k.startswith("ps"):
                for r in self.readers.get(k, ()):
                    if self.ops[r]["eng"] != eng:
                        deps.add(r)
        for k in writes:
            if k in self.last_w:
                deps.add(self.last_w[k])
            for r in self.readers.get(k, ()):
                deps.add(r)
        deps.discard(n)
        for k in reads:
            self.readers.setdefault(k, []).append(n)
        for k in writes:
            self.last_w[k] = n
            self.readers[k] = []
        eidx = self.ecount.get(eng, 0)
        self.ecount[eng] = eidx + 1
        keep = set()
        for d in deps:
            od = self.ops[d]
            if od["dma"] is None and od["eng"] == eng:
                if eng != "pe" and (self.strict or (d in raw and eidx - od["eidx"] <= self.RAW_DIST)):
                    keep.add(d)
            else:
                keep.add(d)
        self.ops.append(dict(eng=eng, fn=_snap(fn), deps=keep, dma=dma, sig=False, seq=None, eidx=eidx))
        return n

    def emit(self, stack):
        nc = self.nc
        ops = self.ops
        for n, op in enumerate(ops):
            for d in op["deps"]:
                od = ops[d]
                if od["dma"] is not None:
                    continue
                od["sig"] = True
        cnt = {}
        for op in ops:
            if op["dma"] is not None:
                key = ("dma", op["dma"])
                cnt[key] = cnt.get(key, 0) + 16
                op["seq"] = (key, cnt[key])
            elif op["sig"]:
                key = ("eng", op["eng"])
                cnt[key] = cnt.get(key, 0) + 1
                op["seq"] = (key, cnt[key])
        sems = {}
        for i, key in enumerate(cnt):
            sems[key] = stack.enter_context(nc.semaphore("sem%d" % i))
        per_eng = {e: [] for e in self.ENG}
        for n, op in enumerate(ops):
            per_eng[op["eng"]].append(n)
        block = stack.enter_context(nc.Block())
        same = self.same

        def run(engname, engobj):
            known = {}
            for n in per_eng[engname]:
                op = ops[n]
                need = {}
                for d in op["deps"]:
                    od = ops[d]
                    if od["seq"] is None:
                        continue
                    key, v = od["seq"]
                    if known.get(key, 0) >= v:
                        continue
                    need[key] = max(need.get(key, 0), v)
                for key, v in need.items():
                    engobj.wait_ge(sems[key], v)
                    known[key] = v
                ins = op["fn"](engobj)
                if op["seq"] is not None:
                    key, v = op["seq"]
                    ins.then_inc(sems[key], 16 if op["dma"] is not None else 1)
            if engname == "sp":
                for key, v in cnt.items():
                    if key[0] == "dma" and known.get(key, 0) < v:
                        engobj.wait_ge(sems[key], v)

        @block.tensor
        def _(e):
            run("pe", e)

        @block.scalar
        def _(e):
            run("act", e)

        @block.vector
        def _(e):
            run("dve", e)

        @block.gpsimd
        def _(e):
            run("pool", e)

        @block.sync
        def _(e):
            run("sp", e)


def build_program(n_prompt_super=PB * NBLK // NT, with_sample=True, stop_after=None):
    nc = bass.Bass("TRN2", target_bir_lowering=False)
    st = ExitStack()

    def din(name, shape, dt=F32):
        return nc.dram_tensor(name, list(shape), dt, kind="ExternalInput").ap()

    def dout(name, shape, dt=F32):
        return nc.dram_tensor(name, list(shape), dt, kind="ExternalOutput").ap()

    x_p = din("x_p", [PB, SEQ, D])
    x_s = din("x_s", [SB * TS, D])
    c_dk = din("c_dk", [SB, SEQ, 512])
    c_dv = din("c_dv", [SB, SEQ, 512])
    c_sk = din("c_sk", [SB, SEQ, 512])
    c_sv = din("c_sv", [SB, SEQ, 512])
    st_cv = din("st_cv", [128, 2 * NCH, SB, 2])
    w_in = din("w_in", [D, 5120])
    w_bra = din("w_bra", [512, D])
    w_brb = din("w_brb", [512, D])
    w_out = din("w_out", [D, D])
    w_up = din("w_up", [D, 2 * DFF])
    w_down = din("w_down", [DFF, D])
    g_attn = din("g_attn", [128, 8])
    g_ffn = din("g_ffn", [128, 8])
    g_fin = din("g_fin", [D])
    subln = din("subln", [128])
    lam4 = din("lam4", [4 * 64])
    conv_wt = din("conv_wt", [128, 2 * NCH, 3])
    conv_bt = din("conv_bt", [128, 2 * NCH])
    c_ident = din("c_ident", [128, 128], BF16)
    c_tri = din("c_tri", [128, 128], BF16)
    c_onesn = din("c_onesn", [128, 128], BF16)
    c_nmsb = din("c_nmsb", [128, 512], BF16)
    c_nmda = din("c_nmda", [128, 512], BF16)
    c_nmsb16 = din("c_nmsb16", [16, 64], BF16)
    c_cosp = din("c_cosp", [128, NBLK, 32])
    c_sinp = din("c_sinp", [128, NBLK, 32])
    c_coss = din("c_coss", [SB * TS, 32])
    c_sins = din("c_sins", [SB * TS, 32])

    y_p = dout("y_p", [PB, SEQ, D])
    y_s = dout("y_s", [SB * TS, D])
    p_dk = dout("p_dk", [PB, SEQ, 512])
    p_dv = dout("p_dv", [PB, SEQ, 512])
    p_sk = dout("p_sk", [PB, SEQ, 512])
    p_sv = dout("p_sv", [PB, SEQ, 512])
    p_cv = dout("p_cv", [PB, 128, 2 * NCH, 2])
    s_dk = dout("s_dk", [SB * TS, 512])
    s_dv = dout("s_dv", [SB * TS, 512])
    s_sk = dout("s_sk", [SB * TS, 512])
    s_sv = dout("s_sv", [SB * TS, 512])
    s_cv = dout("s_cv", [128, 2 * NCH, SB, 2])

    def sb(name, shape, dt):
        return st.enter_context(nc.sbuf_tensor(name, list(shape), dt))

    HCH = 12
    SCW = 520
    ident = sb("ident", [128, 128], BF16)
    tri = sb("tri", [128, 128], BF16)
    onesn = sb("onesn", [128, 128], BF16)
    nmsb = sb("nmsb", [128, 512], BF16)
    nmda = sb("nmda", [128, 512], BF16)
    nmsb16 = sb("nmsb16", [16, 64], BF16)
    one1 = sb("one1", [128, 1], F32)
    epsc = sb("epsc", [128, 1], F32)
    gAf = sb("gAf", [128, 8], F32)
    gFf = sb("gFf", [128, 8], F32)
    gO = sb("gO", [128, D], F32)
    gsub = sb("gsub", [128, 128], F32)
    l4 = sb("l4", [128, 4, 64], F32)
    lw = sb("lw", [128, 8], F32)
    cosp = sb("cosp", [128, NBLK, 32], F32)
    sinp = sb("sinp", [128, NBLK, 32], F32)
    coss = sb("coss", [SB * TS, 32], F32)
    sins = sb("sins", [SB * TS, 32], F32)
    cw = sb("cw", [128, 2 * NCH, 3], F32)
    cb = sb("cb", [128, 2 * NCH], F32)
    CBp = sb("CBp", [128, 2 * NCH, 1, 2], F32)
    CBs = sb("CBs", [128, 2 * NCH, SB, 2], F32)
    KaT = sb("KaT", [128, 4, SEQ], BF16)
    KbT = sb("KbT", [128, 4, SEQ], BF16)
    Va = sb("Va", [128, NBLK, 4, 130], BF16)
    Vb = sb("Vb", [128, NBLK, 512], BF16)
    KaTn = sb("KaTn", [128, 4, SB * TS], BF16)
    KbTn = sb("KbTn", [128, 4, SB * TS], BF16)
    X = sb("X", [128, NT, D], F32)
    xnT = sb("xnT", [128, 8, TOK], BF16)
    qaT = sb("qaT", [128, 4, TOK], BF16)
    qbT = sb("qbT", [128, 4, TOK], BF16)
    oaT = sb("oaT", [128, 4, TOK], BF16)
    obT = sb("obT", [128, 4, TOK], BF16)
    mT = sb("mT", [128, 8 * TOK], BF16)
    hT = sb("hT", [128, HCH * TOK], BF16)
    mTv = mT[:, :].rearrange("p (c t) -> p c t", c=8)
    Van = mT[0:TS, 0:SB * 4 * 130].rearrange("p (s h e) -> p s h e", s=SB, h=4)
    Vbn = hT[0:TS, 4096:4096 + SB * 512].rearrange("p (s f) -> p s f", s=SB)
    zst = sb("zst", [128, 2056], F32)
    fz = sb("fz", [128, 1], F32)
    ZCO = (0, 514, 1028, 1540)
    stg = dict(i=0)
    xnb = sb("xnb", [128, D], BF16)
    sm = sb("sm", [128, 32], F32)
    rA = sb("rA", [128, 8, 32], F32)
    rB = sb("rB", [128, 8, 32], F32)
    tbf2 = sb("tbf", [128, 2, 512], BF16)
    PA = sb("PA", [128, 4, 512], BF16)
    SP = sb("SP", [128, 4, 512], BF16)
    SPS = sb("SPS", [128, 2, 512], BF16)
    SCR = sb("SCR", [128, 4, SCW], F32)
    oa = sb("oa", [128, 4, 128], F32)
    oan = sb("oan", [128, 4, 128], BF16)
    WS = [sb("ws%d" % i, [128, 4096], BF16) for i in range(NSLOT)]

    PSALL = st.enter_context(nc.psum_tensor("psall", [128, 8 * 512], F32))
    PS = [PSALL[:, i * 512:(i + 1) * 512] for i in range(8)]
    PSB = [p.bitcast(BF16) for p in PS]

    def ps2(b0, nk, w):
        return PSALL[0:nk, b0 * 512:(b0 + 2) * 512].rearrange("p (b w) -> p b w", b=2)[:, :, 0:w]

    S = Sched(nc)
    A = S.add

    def ld(eng, dst, src, key):
        A(eng, lambda e: e.dma_start(out=dst, in_=src), writes=[key], dma=key)

    ld("sp", ident[:], c_ident[:, :], "ident")
    ld("sp", tri[:], c_tri[:, :], "tri")
    ld("sp", onesn[:], c_onesn[:, :], "onesn")
    ld("sp", nmsb[:], c_nmsb[:, :], "nmsb")
    ld("sp", nmda[:], c_nmda[:, :], "nmda")
    ld("sp", nmsb16[:], c_nmsb16[:, :], "nmsb16")
    ld("sp", gAf[:], g_attn[:, :], "gAf")
    ld("sp", gFf[:], g_ffn[:, :], "gFf")
    ld("sp", gO[:], g_fin.partition_broadcast(128), "gO")
    ld("sp", gsub[:], subln.partition_broadcast(128), "gsub")
    ld("sp", l4[:].rearrange("p a b -> p (a b)"), lam4.partition_broadcast(128), "l4")
    ld("sp", cosp[:], c_cosp[:, :, :], "cosp")
    ld("sp", sinp[:], c_sinp[:, :, :], "sinp")
    ld("sp", coss[:], c_coss[:, :], "coss")
    ld("sp", sins[:], c_sins[:, :], "sins")
    ld("sp", cw[:], conv_wt[:, :, :], "cw")
    ld("sp", cb[:], conv_bt[:, :], "cb")
    ld("sp", CBs[:], st_cv[:, :, :, :], "CBs")
    A("pool", lambda e: e.memset(one1[:], 1.0), writes=["one1"])
    A("pool", lambda e: e.memset(epsc[:], EPS), writes=["epsc"])
    A("pool", lambda e: e.memset(Va[:, :, :, 128:130], 1.0), writes=["Va"])
    A("dve", lambda e: e.tensor_scalar(out=gsub[:], in0=gsub[:], scalar1=0.8, scalar2=None, op0=ALU.mult),
      reads=["gsub"], writes=["gsub"])
    A("dve", lambda e: e.tensor_tensor(out=l4[:, 0, :], in0=l4[:, 0, :], in1=l4[:, 1, :], op=ALU.mult), reads=["l4"], writes=["l4"])
    A("dve", lambda e: e.tensor_tensor(out=l4[:, 2, :], in0=l4[:, 2, :], in1=l4[:, 3, :], op=ALU.mult), reads=["l4"], writes=["l4"])
    A("dve", lambda e: e.reduce_sum(out=lw[:, 0:1], in_=l4[:, 0, :], axis=AX.X), reads=["l4"], writes=["lw"])
    A("dve", lambda e: e.reduce_sum(out=lw[:, 1:2], in_=l4[:, 2, :], axis=AX.X), reads=["l4"], writes=["lw"])
    A("act", lambda e: e.activation(out=lw[:, 2:4], in_=lw[:, 0:2], func=AF.Exp), reads=["lw"], writes=["lw"])
    A("dve", lambda e: e.tensor_tensor(out=lw[:, 4:5], in0=lw[:, 3:4], in1=lw[:, 2:3], op=ALU.subtract), reads=["lw"], writes=["lw"])
    A("dve", lambda e: e.tensor_scalar(out=lw[:, 4:5], in0=lw[:, 4:5], scalar1=-0.2, scalar2=None, op0=ALU.add),
      reads=["lw"], writes=["lw"])
    neglam = lw[:, 4:5]

    def w_cols(w, c0, ncols):
        return w[:, c0:c0 + ncols].rearrange("(k p) n -> p k n", p=128)

    def w_rows(w, r0, nk):
        return w[r0:r0 + nk * 128, :].rearrange("(k p) n -> p k n", p=128)

    specs = []
    for g in (0, 1, 2, 4, 5, 3):
        specs.append(("in%d" % g, [(0, 8, 512, w_cols(w_in, g * 512, 512))]))
    specs.append(("wa", [(0, 4, 1024, w_rows(w_bra, 0, 4))]))
    specs.append(("ga0", [(0, 8, 512, w_cols(w_in, 3072, 512))]))
    specs.append(("ga1", [(0, 8, 512, w_cols(w_in, 3584, 512))]))
    specs.append(("wb", [(0, 4, 1024, w_rows(w_brb, 0, 4))]))
    specs.append(("gb0", [(0, 8, 512, w_cols(w_in, 4096, 512))]))
    specs.append(("gb1", [(0, 8, 512, w_cols(w_in, 4608, 512))]))
    specs.append(("wo0", [(0, 8, 512, w_cols(w_out, 0, 512))]))
    specs.append(("wo1", [(0, 8, 512, w_cols(w_out, 512, 512))]))
    for hf in range(2):
        for jp in (range(0, 6) if hf == 0 else range(6, 11)):
            specs.append(("up%d" % jp, [(0, 8, 256, w_cols(w_up, jp * 256, 256)),
                                       (2048, 8, 256, w_cols(w_up, DFF + jp * 256, 256))]))
        for pi in (range(0, 3) if hf == 0 else range(3, 6)):
            nkp = 4 if pi < 5 else 2
            specs.append(("dn%d" % pi, [(0, nkp, 1024, w_rows(w_down, pi * 512, nkp))]))
    NSPEC = len(specs)
    wsc = nc.dram_tensor("wsc", [NSPEC, 128, 4096], BF16).ap()
    plen = [sum(nk * ncols for (_, nk, ncols, _) in sp[1]) for sp in specs]
    spec_idx = {name: i for i, (name, _) in enumerate(specs)}
    n_super_total = n_prompt_super + (1 if with_sample else 0)
    wst = dict(free=list(range(NSLOT)), pending=0, loaded={}, sup=0)

    def wissue(gi, slot):
        key = "ws%d" % slot
        t = WS[slot]
        li = gi % NSPEC
        if gi < NSPEC or stop_after is not None:
            for (off, nk, ncols, ap) in specs[li][1]:
                dst = t[:, off:off + nk * ncols].rearrange("p (k n) -> p k n", k=nk)
                A("pool", lambda e, dst=dst, ap=ap: e.dma_start(out=dst, in_=ap), writes=[key], dma=key)
            if n_super_total > 1 and stop_after is None:
                A("sp", lambda e, t=t, li=li: e.dma_start(out=wsc[li, :, 0:plen[li]], in_=t[:, 0:plen[li]]),
                  reads=[key], writes=[("wsc", li)], dma=("wst", slot))
        else:
            A("pool", lambda e, t=t, li=li: e.dma_start(out=t[:, 0:plen[li]], in_=wsc[li, :, 0:plen[li]]),
              reads=[("wsc", li)], writes=[key], dma=key)

    def wpump():
        total = NSPEC * n_super_total
        if stop_after is not None:
            return
        while wst["free"] and wst["pending"] < total:
            slot = wst["free"].pop(0)
            gi = wst["pending"]
            wst["pending"] += 1
            wissue(gi, slot)
            wst["loaded"][gi] = slot

    def wget(name):
        gi = wst["sup"] * NSPEC + spec_idx[name]
        if gi not in wst["loaded"]:
            slot = wst["free"].pop(0)
            wissue(gi, slot)
            wst["loaded"][gi] = slot
        slot = wst["loaded"][gi]
        return WS[slot], "ws%d" % slot

    def wdone(name):
        gi = wst["sup"] * NSPEC + spec_idx[name]
        wst["free"].append(wst["loaded"].pop(gi))
        wpump()

    def rmsnorm_to_xnT(n, s, gf, gkey):
        xs = X[0:n, s, :]
        A("act", lambda e: e.activation(out=xnb[0:n, :], in_=xs, func=AF.Square, accum_out=sm[0:n, 0:1]),
          reads=[("X", s)], writes=["xnb", "sm0"])
        A("act", lambda e: e.activation(out=sm[0:n, 1:2], in_=sm[0:n, 0:1], func=AF.Ln, scale=1.0 / D, bias=epsc[0:n, 0:1]),
          reads=["sm0", "epsc"], writes=["sm1"])
        A("act", lambda e: e.activation(out=sm[0:n, 1:2], in_=sm[0:n, 1:2], func=AF.Exp, scale=-0.5), reads=["sm1"], writes=["sm1"])
        A("dve", lambda e: e.tensor_scalar(out=xnb[0:n, :], in0=xs, scalar1=sm[0:n, 1:2], scalar2=None, op0=ALU.mult),
          reads=[("X", s), "sm1"], writes=["xnb"])
        for c in range(8):
            A("pe", lambda e, c=c: e.transpose(out=PSB[7][:, c * 128:c * 128 + n], in_=xnb[0:n, c * 128:(c + 1) * 128],
                                               identity=ident[0:n, 0:n]),
              reads=["xnb", "ident"], writes=["ps7"])
        A("dve", lambda e: e.tensor_tensor(out=xnT[:, :, s * 128:s * 128 + n],
                                           in0=PSB[7][:, 0:1024].rearrange("p (c t) -> p c t", c=8)[:, :, 0:n],
                                           in1=gf[:, :].unsqueeze(2).to_broadcast([128, 8, n]), op=ALU.mult),
          reads=["ps7", gkey], writes=[("xnT", s)])

    def transpose4(src_bf, n, dst_fn, dst_keys, src_key):
        for h in range(4):
            A("pe", lambda e, h=h: e.transpose(out=PSB[6][:, h * 128:h * 128 + n], in_=src_bf[0:n, h * 128:(h + 1) * 128],
                                               identity=ident[0:n, 0:n]),
              reads=[src_key, "ident"], writes=["ps6"])
        A("act", lambda e: e.copy(out=dst_fn(), in_=PSB[6][:, 0:512].rearrange("p (c t) -> p c t", c=4)[:, :, 0:n]),
          reads=["ps6"], writes=dst_keys)

    def rope(zp, n, cos, sin, dst, rkeys, wkeys):
        zv = zp.rearrange("p (g t d) -> p g t d", g=8, t=2)
        dv = dst.rearrange("p (g t d) -> p g t d", g=8, t=2)
        cb_ = cos.unsqueeze(1).to_broadcast([n, 8, 32])
        sb_ = sin.unsqueeze(1).to_broadcast([n, 8, 32])
        x1 = zv[:, :, 0, :]
        x2 = zv[:, :, 1, :]
        A("dve", lambda e: e.tensor_tensor(out=rA[0:n], in0=x1, in1=cb_, op=ALU.mult), reads=rkeys, writes=["rA"])
        A("dve", lambda e: e.tensor_tensor(out=rB[0:n], in0=x2, in1=sb_, op=ALU.mult), reads=rkeys, writes=["rB"])
        A("dve", lambda e: e.tensor_tensor(out=dv[:, :, 0, :], in0=rA[0:n], in1=rB[0:n], op=ALU.subtract),
          reads=["rA", "rB"], writes=wkeys)
        A("dve", lambda e: e.tensor_tensor(out=rA[0:n], in0=x2, in1=cb_, op=ALU.mult), reads=rkeys, writes=["rA"])
        A("dve", lambda e: e.tensor_tensor(out=rB[0:n], in0=x1, in1=sb_, op=ALU.mult), reads=rkeys, writes=["rB"])
        A("dve", lambda e: e.tensor_tensor(out=dv[:, :, 1, :], in0=rA[0:n], in1=rB[0:n], op=ALU.add),
          reads=["rA", "rB"], writes=wkeys)

    def attention(nq, qcol, blocks):
        W4 = 4 * nq
        nb = len(blocks)
        qk = qcol // 128

        def d_qk(bi):
            blk = blocks[bi]
            nk = blk["nk"]
            stt = bi % 2
            for c in range(2):
                bk = 2 * stt + c
                bank = PS[bk]
                pkey = "ps%d" % bk
                first = True
                if blk["diag"] and nq == 128:
                    A("pe", lambda e, bank=bank, nk=nk: e.matmul(bank[0:nk, 0:W4], lhsT=ident[0:nk, 0:nk], rhs=nmda[0:nk, 0:W4],
                                                                start=True, stop=False, skip_group_check=True),
                      reads=["ident", "nmda"], writes=[pkey])
                    first = False
                for h in range(4):
                    A("pe", lambda e, bank=bank, nk=nk, h=h, c=c, blk=blk, fl=(first and h == 0):
                      e.matmul(bank[0:nk, h * nq:(h + 1) * nq], lhsT=blk["ka"](h)[64 * c:64 * c + 64, :],
                               rhs=qaT[64 * c:64 * c + 64, h, qcol:qcol + nq], start=fl, stop=(h == 3), skip_group_check=True),
                      reads=blk["kka"] + [("qaT", qk)], writes=[pkey])
            b0 = 2 * stt
            A("act", lambda e, nk=nk, b0=b0: e.activation(out=PA[0:nk, b0:b0 + 2, 0:W4], in_=ps2(b0, nk, W4),
                                                          func=AF.Exp, scale=0.125),
              reads=["ps%d" % b0, "ps%d" % (b0 + 1)], writes=[("PA", b0), ("PA", b0 + 1)])

        def d_pv(bi):
            blk = blocks[bi]
            nk = blk["nk"]
            stt = bi % 2
            for c in range(2):
                bk = 2 * stt + c
                for h in range(4):
                    ob = 4 + 2 * c + h // 2
                    A("pe", lambda e, nk=nk, h=h, bk=bk, ob=ob, blk=blk, bi=bi:
                      e.matmul(PS[ob][0:nq, (h % 2) * 129:(h % 2) * 129 + 129], lhsT=PA[0:nk, bk, h * nq:(h + 1) * nq],
                               rhs=blk["va"](h), start=(bi == 0 and h % 2 == 0), stop=(bi == nb - 1), skip_group_check=True),
                      reads=blk["kva"] + [("PA", bk)], writes=["ps%d" % ob])

        for bi in range(nb + 1):
            if bi < nb:
                d_qk(bi)
            if bi >= 1:
                d_pv(bi - 1)

        def epilogue():
            for c in range(2):
                for hp in range(2):
                    ob = 4 + 2 * c + hp
                    A("dve", lambda e, ob=ob, c=c, hp=hp: e.reciprocal(
                        out=sm[0:nq, 8 + 4 * c + 2 * hp:8 + 4 * c + 2 * hp + 2],
                        in_=PS[ob][0:nq, 0:258].rearrange("p (a b) -> p a b", a=2)[:, :, 128]),
                      reads=["ps%d" % ob], writes=["smrr"])
            A("dve", lambda e: e.tensor_scalar(out=sm[0:nq, 12:16], in0=sm[0:nq, 12:16], scalar1=neglam[0:nq, :], scalar2=None,
                                               op0=ALU.mult), reads=["smrr", "lw"], writes=["smrr"])
            for h in range(4):
                o0 = 4 + h // 2
                o1 = 6 + h // 2
                col = (h % 2) * 129
                A("dve", lambda e, h=h, o0=o0, col=col: e.tensor_scalar(out=oa[0:nq, h, :], in0=PS[o0][0:nq, col:col + 128],
                                                                      scalar1=sm[0:nq, 8 + h:9 + h], scalar2=None, op0=ALU.mult),
                  reads=["ps%d" % o0, "smrr"], writes=["oa"])
                A("dve", lambda e, h=h, o1=o1, col=col: e.scalar_tensor_tensor(out=oa[0:nq, h, :], in0=PS[o1][0:nq, col:col + 128],
                                                                             scalar=sm[0:nq, 12 + h:13 + h], in1=oa[0:nq, h, :],
                                                                             op0=ALU.mult, op1=ALU.add),
                  reads=["ps%d" % o1, "smrr", "oa"], writes=["oa"])
                A("dve", lambda e, h=h: e.scalar_tensor_tensor(out=oan[0:nq, h, :], in0=oa[0:nq, h, :], scalar=1.0, in1=oa[0:nq, h, :],
                                                              op0=ALU.mult, op1=ALU.mult, accum_out=sm[0:nq, 16 + h:17 + h]),
                  reads=["oa"], writes=["oan", "smss"])
            A("act", lambda e: e.activation(out=sm[0:nq, 20:24], in_=sm[0:nq, 16:20], func=AF.Ln, scale=1.0 / 128, bias=epsc[0:nq, 0:1]),
              reads=["smss", "epsc"], writes=["smrs"])
            A("act", lambda e: e.activation(out=sm[0:nq, 20:24], in_=sm[0:nq, 20:24], func=AF.Exp, scale=-0.5), reads=["smrs"], writes=["smrs"])
            for h in range(4):
                A("dve", lambda e, h=h: e.scalar_tensor_tensor(out=oan[0:nq, h, :], in0=oa[0:nq, h, :], scalar=sm[0:nq, 20 + h:21 + h],
                                                              in1=gsub[0:nq, :], op0=ALU.mult, op1=ALU.mult),
                  reads=["oa", "smrs", "gsub"], writes=["oan"])

        rb = list(reversed(blocks))

        def s_Q(bi):
            blk = rb[bi]
            nk = blk["nk"]
            stt = bi % 2
            for par in range(2):
                bk = 2 * stt + par
                bank = PS[bk]
                pkey = "ps%d" % bk
                first = True
                if blk["diag"]:
                    nm_t = nmsb if nq == 128 else nmsb16
                    A("pe", lambda e, bank=bank, nk=nk, nm_t=nm_t: e.matmul(bank[0:nk, 0:W4], lhsT=ident[0:nk, 0:nk],
                                                                           rhs=nm_t[0:nk, 0:W4], start=True, stop=False,
                                                                           skip_group_check=True),
                      reads=["ident", "nmsb", "nmsb16"], writes=[pkey])
                    first = False
                for p in range(4):
                    A("pe", lambda e, bank=bank, nk=nk, p=p, par=par, blk=blk, fl=(first and p == 0):
                      e.matmul(bank[0:nk, p * nq:(p + 1) * nq], lhsT=blk["kb"](p)[64 * par:64 * par + 64, :],
                               rhs=qbT[64 * par:64 * par + 64, p, qcol:qcol + nq], start=fl, stop=False, skip_group_check=True),
                      reads=blk["kkb"] + [("qbT", qk)], writes=[pkey])

        def s_E(bi):
            nk = rb[bi]["nk"]
            b0 = 2 * (bi % 2)
            A("act", lambda e, nk=nk, b0=b0: e.activation(out=SCR[0:nk, b0:b0 + 2, 0:W4], in_=ps2(b0, nk, W4), func=AF.Exp),
              reads=["ps%d" % b0, "ps%d" % (b0 + 1)], writes=[("SCR", b0), ("SCR", b0 + 1)])

        def s_L(bi):
            nk = rb[bi]["nk"]
            b0 = 2 * (bi % 2)
            A("act", lambda e, nk=nk, b0=b0: e.activation(out=SP[0:nk, b0:b0 + 2, 0:W4], in_=SCR[0:nk, b0:b0 + 2, 0:W4], func=AF.Ln,
                                                          bias=one1[0:nk, 0:1], scale=1.0),
              reads=[("SCR", b0), ("SCR", b0 + 1), "one1"], writes=[("SP", b0), ("SP", b0 + 1)])

        def s_C(bi):
            nk = rb[bi]["nk"]
            stt = bi % 2
            last = (bi == nb - 1)
            for par in range(2):
                bk = 2 * stt + par
                bank = PS[bk]
                pkey = "ps%d" % bk
                if bi > 0:
                    A("pe", lambda e, bank=bank, nk=nk, par=par: e.matmul(bank[0:nk, 0:W4], lhsT=onesn[:, 0:nk],
                                                                         rhs=SPS[:, par, 0:W4], start=False, stop=False,
                                                                         skip_group_check=True),
                      reads=["onesn", ("SPS", par)], writes=[pkey])
            for par in range(2):
                bk = 2 * stt + par
                bank = PS[bk]
                pkey = "ps%d" % bk
                A("pe", lambda e, bank=bank, nk=nk, bk=bk: e.matmul(bank[0:nk, 0:W4], lhsT=tri[0:nk, 0:nk],
                                                                   rhs=SP[0:nk, bk, 0:W4], start=False, stop=True,
                                                                   skip_group_check=True),
                  reads=["tri", ("SP", bk)], writes=[pkey])
            b0 = 2 * stt
            if not last:
                spk = [("SP", b0), ("SP", b0 + 1)]
                spsk = [("SPS", 0), ("SPS", 1)]
                if bi == 0:
                    if nk < 128:
                        A("pool", lambda e: e.memset(SPS[:, :, 0:W4], 0.0), writes=spsk)
                    A("dve", lambda e, nk=nk, b0=b0: e.tensor_copy(out=SPS[0:nk, :, 0:W4], in_=SP[0:nk, b0:b0 + 2, 0:W4]),
                      reads=spk, writes=spsk)
                else:
                    A("dve", lambda e, nk=nk, b0=b0: e.tensor_tensor(out=SPS[0:nk, :, 0:W4], in0=SPS[0:nk, :, 0:W4],
                                                                    in1=SP[0:nk, b0:b0 + 2, 0:W4], op=ALU.add),
                      reads=spk + spsk, writes=spsk)

        def s_F(bi):
            nk = rb[bi]["nk"]
            b0 = 2 * (bi % 2)
            A("act", lambda e, nk=nk, b0=b0: e.activation(out=PA[0:nk, b0:b0 + 2, 0:W4], in_=ps2(b0, nk, W4), func=AF.Exp),
              reads=["ps%d" % b0, "ps%d" % (b0 + 1)], writes=[("PA", b0), ("PA", b0 + 1)])

        def s_V(bi):
            blk = rb[bi]
            nk = blk["nk"]
            stt = bi % 2
            last = (bi == nb - 1)
            for par in range(2):
                bk = 2 * stt + par
                for p in range(4):
                    h = 2 * p + par
                    A("pe", lambda e, nk=nk, h=h, p=p, par=par, bk=bk, blk=blk, bi=bi, last=last:
                      e.matmul(PS[5][64 * par:64 * par + 64, p * nq:(p + 1) * nq], lhsT=blk["vb"](h),
                               rhs=PA[0:nk, bk, p * nq:(p + 1) * nq], start=(bi == 0 and p == 0), stop=last, skip_group_check=True),
                      reads=blk["kvb"] + [("PA", bk)], writes=["ps5"])

        s_Q(0)
        s_E(0)
        s_L(0)
        for b in range(nb):
            if b + 1 < nb:
                s_Q(b + 1)
            s_C(b)
            if b >= 1:
                s_V(b - 1)
            if b + 1 < nb:
                s_E(b + 1)
            s_F(b)
            if b + 1 < nb:
                s_L(b + 1)
            if b == 0:
                epilogue()
        s_V(nb - 1)
        A("dve", lambda e: e.tensor_copy(out=obT[:, :, qcol:qcol + nq],
                                         in_=PS[5][:, 0:W4].rearrange("p (c t) -> p c t", c=4)),
          reads=["ps5"], writes=[("obT", qk)])
        for h in range(4):
            A("pe", lambda e, h=h: e.transpose(out=PSB[4][:, h * 128:h * 128 + nq], in_=oan[0:nq, h, :],
                                               identity=ident[0:nq, 0:nq]),
              reads=["oan", "ident", "oa"], writes=["ps4"])
        A("dve", lambda e: e.tensor_copy(out=oaT[:, :, qcol:qcol + nq],
                                         in_=PSB[4][:, 0:512].rearrange("p (c t) -> p c t", c=4)[:, :, 0:nq]),
          reads=["ps4"], writes=[("oaT", qk)])

    def super_tile(tiles, sample, next_tiles=None, x_loaded=False):
        ntile = len(tiles)
        ntok = sum(t["n"] for t in tiles)
        for s, t in enumerate(tiles):
            n = t["n"]
            if not x_loaded:
                A("sp", lambda e, s=s, t=t, n=n: e.dma_start(out=X[0:n, s, :], in_=t["x"]), writes=[("X", s)], dma=("X", s))
            rmsnorm_to_xnT(n, s, gAf, "gAf")
        if stop_after == "A1":
            return
        xkeys = [("xnT", s) for s in range(ntile)]
        cskeys = ["coss", "sins"] if sample else ["cosp", "sinp"]
        pend = []

        def flush_pend():
            while pend:
                pend.pop(0)()

        for g in (0, 1, 2, 4, 5, 3):
            wt, wkey = wget("in%d" % g)
            wv = wt[:, :].rearrange("p (k n) -> p k n", k=8)
            if g == 3:
                for p in range(4):
                    bank = PS[p % 4]
                    pkey = "ps%d" % (p % 4)
                    for k in range(8):
                        A("pe", lambda e, bank=bank, k=k, p=p: e.matmul(bank[:, 0:ntok], lhsT=wv[:, k, p * 128:(p + 1) * 128],
                                                                       rhs=xnT[:, k, 0:ntok], start=(k == 0), stop=(k == 7)),
                          reads=[wkey] + xkeys, writes=[pkey])
                    flush_pend()
                    A("act", lambda e, bank=bank, p=p: e.activation(out=qbT[:, p, 0:ntok], in_=bank[:, 0:ntok], func=AF.Copy, scale=0.125),
                      reads=[pkey], writes=[("qbT", s) for s in range(ntile)])
                wdone("in3")
                continue
            for s, t in enumerate(tiles):
                n = t["n"]
                bk = (s + g) % 4
                bank = PS[bk]
                pkey = "ps%d" % bk
                for k in range(8):
                    A("pe", lambda e, bank=bank, k=k, s=s, n=n: e.matmul(bank[0:n, :], lhsT=xnT[:, k, s * 128:s * 128 + n],
                                                                        rhs=wv[:, k, :], start=(k == 0), stop=(k == 7)),
                      reads=[wkey, ("xnT", s)], writes=[pkey])
                flush_pend()
                tbf = tbf2[:, s % 2, :]
                tbk = ("tbf", s % 2)
                if g == 0:
                    rope(bank[0:n, :], n, t["cos"], t["sin"], tbf[0:n, :], [pkey] + cskeys, [tbk])
                    pend.append(lambda s=s, n=n, tbf=tbf, tbk=tbk: transpose4(
                        tbf, n, lambda s=s, n=n: qaT[:, :, s * 128:s * 128 + n], [("qaT", s)], tbk))
                else:
                    zr = stg["i"] % 4
                    stg["i"] += 1
                    zs = zst[:, zr * 512:(zr + 1) * 512]
                    zk = ("zst", zr)
                if g == 0:
                    pass
                elif g == 1:
                    rope(bank[0:n, :], n, t["cos"], t["sin"], zs[0:n, :], [pkey] + cskeys, [zk])
                    A("sp", lambda e, t=t, n=n, zs=zs: e.dma_start(out=t["dk"], in_=zs[0:n, :]), reads=[zk], dma=zk)
                    A("act", lambda e, n=n, tbf=tbf, zs=zs: e.copy(out=tbf[0:n, :], in_=zs[0:n, :]), reads=[zk], writes=[tbk])
                    if sample:
                        pend.append(lambda n=n, tbf=tbf, tbk=tbk: transpose4(tbf, n, lambda: KaTn[:, :, 0:n], ["KaTn"], tbk))
                    else:
                        pend.append(lambda n=n, t=t, tbf=tbf, tbk=tbk: transpose4(
                            tbf, n, lambda t=t: KaT[:, :, t["blk"] * 128:(t["blk"] + 1) * 128], [("ka", t["blk"])], tbk))
                elif g == 2:
                    A("act", lambda e, bank=bank, n=n, zs=zs: e.copy(out=zs[0:n, :], in_=bank[0:n, :]), reads=[pkey], writes=[zk])
                    A("sp", lambda e, t=t, n=n, zs=zs: e.dma_start(out=t["dv"], in_=zs[0:n, :]), reads=[zk], dma=zk)
                    if sample:
                        for j in range(SB):
                            A("pool", lambda e, j=j, zs=zs: e.dma_start(out=Van[:, j, :, 0:128],
                                                                        in_=zs[j * TS:(j + 1) * TS, :].rearrange("p (h d) -> p h d", h=4)),
                              reads=[zk], writes=["Van"], dma="Van")
                    else:
                        A("dve", lambda e, t=t, zs=zs: e.tensor_copy(out=Va[:, t["blk"], :, 0:128],
                                                                     in_=zs[:, :].rearrange("p (h d) -> p h d", h=4)),
                          reads=[zk], writes=[("va", t["blk"])])
                elif g == 4:
                    A("act", lambda e, bank=bank, n=n, zs=zs: e.copy(out=zs[0:n, :], in_=bank[0:n, :]), reads=[pkey], writes=[zk])
                    A("sp", lambda e, t=t, n=n, zs=zs: e.dma_start(out=t["sk"], in_=zs[0:n, :]), reads=[zk], dma=zk)
                    A("dve", lambda e, n=n, tbf=tbf, zs=zs: e.tensor_copy(out=tbf[0:n, :], in_=zs[0:n, :]), reads=[zk], writes=[tbk])
                    if sample:
                        pend.append(lambda n=n, tbf=tbf, tbk=tbk: transpose4(tbf, n, lambda: KbTn[:, :, 0:n], ["KbTn"], tbk))
                    else:
                        pend.append(lambda n=n, t=t, tbf=tbf, tbk=tbk: transpose4(
                            tbf, n, lambda t=t: KbT[:, :, t["blk"] * 128:(t["blk"] + 1) * 128], [("kb", t["blk"])], tbk))
                elif g == 5:
                    A("act", lambda e, bank=bank, n=n, zs=zs: e.copy(out=zs[0:n, :], in_=bank[0:n, :]), reads=[pkey], writes=[zk])
                    A("sp", lambda e, t=t, n=n, zs=zs: e.dma_start(out=t["sv"], in_=zs[0:n, :]), reads=[zk], dma=zk)
                    if sample:
                        for j in range(SB):
                            A("pool", lambda e, j=j, zs=zs: e.dma_start(out=Vbn[:, j, :], in_=zs[j * TS:(j + 1) * TS, :]),
                              reads=[zk], writes=["Vbn"], dma="Vbn")
                    else:
                        A("dve", lambda e, t=t, zs=zs: e.tensor_copy(out=Vb[:, t["blk"], :], in_=zs[:, :]),
                          reads=[zk], writes=[("vb", t["blk"])])
            wdone("in%d" % g)
        flush_pend()
        if stop_after == "A":
            return
        if not sample:
            for s, t in enumerate(tiles):
                blocks = []
                for kb in range(t["blk"] + 1):
                    blocks.append(dict(
                        nk=128, diag=(kb == t["blk"]),
                        ka=lambda h, kb=kb: KaT[:, h, kb * 128:(kb + 1) * 128],
                        va=lambda h, kb=kb: Va[:, kb, h, 0:129],
                        kb=lambda p, kb=kb: KbT[:, p, kb * 128:(kb + 1) * 128],
                        vb=lambda h, kb=kb: Vb[:, kb, h * 64:(h + 1) * 64],
                        kka=[("ka", kb)], kva=[("va", kb)], kkb=[("kb", kb)], kvb=[("vb", kb)]))
                attention(128, s * 128, blocks)
        else:
            A("pool", lambda e: e.memset(Van[:, :, :, 128:130], 1.0), reads=["mT"], writes=["Van"])
            ksts = [(hT[:, 0:4096].rearrange("p (b f) -> p b f", b=8), ["hT"], "kst0"),
                    (X[:, 1:3, :].rearrange("p a b -> p (a b)").bitcast(BF16).rearrange("p (b f) -> p b f", b=8),
                     [("X", 1), ("X", 2)], "kst1")]
            chunks = [(c_dk, KaT, 0), (c_dk, KaT, 1), (c_sk, KbT, 0), (c_sk, KbT, 1)]

            def kload(j, ci):
                csrc, _, half = chunks[ci]
                kst, kkeys, kdma = ksts[ci % 2]
                A("pool", lambda e: e.dma_start(
                    out=kst, in_=csrc[j, half * 1024:(half + 1) * 1024, :].rearrange("(b p) f -> p b f", p=128)),
                  writes=kkeys, dma=kdma)

            kload(0, 0)
            kload(0, 1)
            for j in range(SB):
                allkv = [("kv", kb) for kb in range(NBLK)]
                for h in range(4):
                    A("pool", lambda e, j=j, h=h: e.dma_start(out=Va[:, :, h, 0:128],
                                                              in_=c_dv[j, :, h * 128:(h + 1) * 128].rearrange("(b p) d -> p b d", p=128)),
                      writes=[("va", kb) for kb in range(NBLK)], dma="cva")
                A("pool", lambda e, j=j: e.dma_start(out=Vb[:, :, :], in_=c_sv[j].rearrange("(b p) f -> p b f", p=128)),
                  writes=[("vb", kb) for kb in range(NBLK)], dma="cvb")
                for ci in range(4):
                    if True:
                        csrc, KT, half = chunks[ci]
                        kst, kkeys, kdma = ksts[ci % 2]
                        for b2 in range(4):
                            pb = 6 + (b2 % 2)
                            for bb in range(2):
                                for h in range(4):
                                    A("pe", lambda e, pb=pb, bb=bb, h=h, b2=b2, kst=kst: e.transpose(
                                        out=PSB[pb][:, (bb * 4 + h) * 128:(bb * 4 + h + 1) * 128],
                                        in_=kst[:, b2 * 2 + bb, h * 128:(h + 1) * 128], identity=ident[:, :]),
                                      reads=kkeys + ["ident"], writes=["ps%d" % pb])
                            k0 = (half * 8 + b2 * 2) * 128
                            if b2 % 2 == 0:
                                A("act", lambda e, pb=pb, KT=KT, k0=k0: e.copy(
                                    out=KT[:, :, k0:k0 + 256].rearrange("p h (b t) -> p b h t", b=2),
                                    in_=PSB[pb][:, 0:1024].rearrange("p (b h t) -> p b h t", b=2, h=4)),
                                  reads=["ps%d" % pb], writes=[(("ka" if KT is KaT else "kb"), kb) for kb in range(NBLK)])
                            else:
                                A("dve", lambda e, pb=pb, KT=KT, k0=k0: e.tensor_copy(
                                    out=KT[:, :, k0:k0 + 256].rearrange("p h (b t) -> p b h t", b=2),
                                    in_=PSB[pb][:, 0:1024].rearrange("p (b h t) -> p b h t", b=2, h=4)),
                                  reads=["ps%d" % pb], writes=[(("ka" if KT is KaT else "kb"), kb) for kb in range(NBLK)])
                        if ci + 2 < 4:
                            kload(j, ci + 2)
                        elif j + 1 < SB:
                            kload(j + 1, ci - 2)
                blocks = []
                for kb in range(NBLK):
                    blocks.append(dict(
                        nk=128, diag=False,
                        ka=lambda h, kb=kb: KaT[:, h, kb * 128:(kb + 1) * 128],
                        va=lambda h, kb=kb: Va[:, kb, h, 0:129],
                        kb=lambda p, kb=kb: KbT[:, p, kb * 128:(kb + 1) * 128],
                        vb=lambda h, kb=kb: Vb[:, kb, h * 64:(h + 1) * 64],
                        kka=[("ka", kb)], kva=[("va", kb)], kkb=[("kb", kb)], kvb=[("vb", kb)]))
                blocks.append(dict(
                    nk=TS, diag=True,
                    ka=lambda h, j=j: KaTn[:, h, j * TS:(j + 1) * TS],
                    va=lambda h, j=j: Van[:, j, h, 0:129],
                    kb=lambda p, j=j: KbTn[:, p, j * TS:(j + 1) * TS],
                    vb=lambda h, j=j: Vbn[:, j, h * 64:(h + 1) * 64],
                    kka=["KaTn"], kva=["Van"], kkb=["KbTn"], kvb=["Vbn"]))
                attention(TS, j * TS, blocks)
        if stop_after == "B":
            return
        okeys = [("oaT", s) for s in range(ntile)] + [("obT", s) for s in range(ntile)]
        for (bname, gname, srcT, first_pass) in (("wa", "ga", oaT, True), ("wb", "gb", obT, False)):
            wbr, wbrkey = wget(bname)
            wbrv = wbr[:, :].rearrange("p (k n) -> p k n", k=4)
            for gp in range(2):
                wg, wgkey = wget("%s%d" % (gname, gp))
                wgv = wg[:, :].rearrange("p (k n) -> p k n", k=8)
                for cc in range(4):
                    c = gp * 4 + cc
                    b0 = 2 * (c % 4)
                    for k in range(4):
                        A("pe", lambda e, b0=b0, k=k, c=c, wbrv=wbrv, srcT=srcT: e.matmul(
                            PS[b0][:, 0:ntok], lhsT=wbrv[:, k, c * 128:(c + 1) * 128], rhs=srcT[:, k, 0:ntok],
                            start=(k == 0), stop=(k == 3)),
                          reads=[wbrkey] + okeys, writes=["ps%d" % b0])
                    for k in range(8):
                        A("pe", lambda e, b0=b0, k=k, cc=cc, wgv=wgv: e.matmul(
                            PS[b0 + 1][:, 0:ntok], lhsT=wgv[:, k, cc * 128:(cc + 1) * 128], rhs=xnT[:, k, 0:ntok],
                            start=(k == 0), stop=(k == 7)),
                          reads=[wgkey] + xkeys, writes=["ps%d" % (b0 + 1)])
                    si = c % 2
                    A("act", lambda e, b0=b0, si=si: e.activation(out=SCR[:, si, 0:ntok], in_=PS[b0 + 1][:, 0:ntok], func=AF.Sigmoid),
                      reads=["ps%d" % (b0 + 1)], writes=[("SCR", si)])
                    if first_pass:
                        A("dve", lambda e, b0=b0, si=si, c=c: e.tensor_tensor(out=mTv[:, c, 0:ntok], in0=SCR[:, si, 0:ntok],
                                                                             in1=PS[b0][:, 0:ntok], op=ALU.mult),
                          reads=[("SCR", si), "ps%d" % b0], writes=["mT"])
                    else:
                        A("dve", lambda e, b0=b0, si=si: e.tensor_tensor(out=SCR[:, 2 + si, 0:ntok], in0=SCR[:, si, 0:ntok],
                                                                        in1=PS[b0][:, 0:ntok], op=ALU.mult),
                          reads=[("SCR", si), "ps%d" % b0], writes=[("SCR", 2 + si)])
                        A("dve", lambda e, si=si, c=c: e.tensor_tensor(out=mTv[:, c, 0:ntok], in0=mTv[:, c, 0:ntok],
                                                                      in1=SCR[:, 2 + si, 0:ntok], op=ALU.add),
                          reads=[("SCR", 2 + si), "mT"], writes=["mT"])
                wdone("%s%d" % (gname, gp))
            wdone(bname)
        for half in range(2):
            wo, wokey = wget("wo%d" % half)
            wov = wo[:, :].rearrange("p (k n) -> p k n", k=8)
            for s, t in enumerate(tiles):
                n = t["n"]
                bk = (2 * s + half) % 8
                for k in range(8):
                    A("pe", lambda e, bk=bk, k=k, s=s, n=n, wov=wov: e.matmul(PS[bk][0:n, :], lhsT=mTv[:, k, s * 128:s * 128 + n],
                                                                             rhs=wov[:, k, :], start=(k == 0), stop=(k == 7)),
                      reads=[wokey, "mT"], writes=["ps%d" % bk])
                A("dve", lambda e, bk=bk, s=s, n=n, half=half: e.tensor_tensor(out=X[0:n, s, half * 512:(half + 1) * 512],
                                                                             in0=X[0:n, s, half * 512:(half + 1) * 512],
                                                                             in1=PS[bk][0:n, :], op=ALU.add),
                  reads=["ps%d" % bk, ("X", s)], writes=[("X", s)])
            wdone("wo%d" % half)
        if stop_after == "C":
            return
        for s, t in enumerate(tiles):
            rmsnorm_to_xnT(t["n"], s, gFf, "gFf")
        CB = CBs if sample else CBp
        cbkey = "CBs" if sample else "CBp"
        nseq = SB if sample else 1
        L = ntok // nseq

        ZK = [("zst", k) for k in range(4)] + [("zc", k) for k in range(4)]

        def cbuf(cs, i, width):
            if cs == 0:
                return SCR[:, i, 0:width], ("SCR", i)
            return zst[:, ZCO[i]:ZCO[i] + width], ("zc", i)

        A("pool", lambda e: e.memset(fz[:], 0.0), writes=ZK)

        tails = []
        for hf in range(2):
            j0 = 0 if hf == 0 else HCH
            nch = HCH if hf == 0 else NCH - HCH
            hv = hT[:, 0:nch * ntok].rearrange("p (c t) -> p c t", c=nch)
            for jp in (range(0, 6) if hf == 0 else range(6, 11)):
                wu, wukey = wget("up%d" % jp)
                wug = wu[:, 0:2048].rearrange("p (k n) -> p k n", k=8)
                wuv = wu[:, 2048:4096].rearrange("p (k n) -> p k n", k=8)
                for jj in range(2):
                    j = 2 * jp + jj
                    bg = 2 * (j % 4)
                    for (bnk, wv_) in ((bg, wug), (bg + 1, wuv)):
                        for k in range(8):
                            A("pe", lambda e, bnk=bnk, wv_=wv_, k=k, jj=jj: e.matmul(
                                PS[bnk][:, 0:ntok], lhsT=wv_[:, k, jj * 128:(jj + 1) * 128], rhs=xnT[:, k, 0:ntok],
                                start=(k == 0), stop=(k == 7)),
                              reads=[wukey] + xkeys, writes=["ps%d" % bnk])
                    cs = j % 2
                    for (bnk, ui, ai, ch) in ((bg, 0, 2, j), (bg + 1, 1, 3, NCH + j)):
                        ub, uk = cbuf(cs, ui, nseq * (L + 2))
                        ab, ak = cbuf(cs, ai, ntok)
                        uv = ub.rearrange("p (a b) -> p a b", a=nseq)
                        av = ab.rearrange("p (a b) -> p a b", a=nseq)
                        uck = ("uc",) + tuple(uk)
                        A("pool", lambda e, uv=uv, ch=ch: e.tensor_copy(out=uv[:, :, 0:2], in_=CB[:, ch, :, :]),
                          reads=[cbkey], writes=[uck])
                        A("act", lambda e, uv=uv, bnk=bnk: e.copy(out=uv[:, :, 2:L + 2],
                                                                  in_=PS[bnk][:, 0:ntok].rearrange("p (a b) -> p a b", a=nseq)),
                          reads=["ps%d" % bnk], writes=[uk])
                        A("pool", lambda e, uv=uv, ch=ch: e.tensor_copy(out=CB[:, ch, :, :], in_=uv[:, :, L:L + 2]),
                          reads=[uk], writes=[cbkey])
                        A("act", lambda e, av=av, bnk=bnk, ch=ch: e.activation(
                            out=av, in_=PS[bnk][:, 0:ntok].rearrange("p (a b) -> p a b", a=nseq), func=AF.Identity,
                            scale=cw[:, ch, 2:3], bias=cb[:, ch:ch + 1]),
                          reads=["ps%d" % bnk, "cw", "cb"], writes=[ak])
                        A("dve", lambda e, uv=uv, av=av, ch=ch: e.scalar_tensor_tensor(
                            out=av, in0=uv[:, :, 1:L + 1], scalar=cw[:, ch, 1:2], in1=av, op0=ALU.mult, op1=ALU.add),
                          reads=[uk, uck, "cw", ak], writes=[ak])
                        A("dve", lambda e, uv=uv, av=av, ch=ch: e.scalar_tensor_tensor(out=av, in0=uv[:, :, 0:L], scalar=cw[:, ch, 0:1],
                                                                                      in1=av, op0=ALU.mult, op1=ALU.add),
                          reads=[uk, uck, "cw", ak], writes=[ak])
                    gb_, gk = cbuf(cs, 2, ntok)
                    vb_, vk = cbuf(cs, 3, ntok)

                    def tail(gb_=gb_, gk=gk, vb_=vb_, vk=vk, j=j, j0=j0, hv=hv):
                        A("act", lambda e: e.activation(out=gb_, in_=gb_, func=AF.Silu), reads=[gk], writes=[gk])
                        A("dve", lambda e: e.tensor_tensor(out=hv[:, j - j0, :], in0=gb_, in1=vb_, op=ALU.mult),
                          reads=[gk, vk], writes=[("hT", j - j0)])
                    while tails:
                        tails.pop(0)()
                    tails.append(tail)
                wdone("up%d" % jp)
            while tails:
                tails.pop(0)()
            for pi in (range(0, 3) if hf == 0 else range(3, 6)):
                nkp = 4 if pi < 5 else 2
                wd, wdkey = wget("dn%d" % pi)
                wdv = wd[:, 0:nkp * 1024].rearrange("p (k n) -> p k n", k=nkp)
                for s, t in enumerate(tiles):
                    n = t["n"]
                    for half in range(2):
                        bk = 2 * s + half
                        for kk in range(nkp):
                            j = pi * 4 + kk
                            A("pe", lambda e, bk=bk, kk=kk, j=j, j0=j0, nch=nch, s=s, n=n, half=half, wdv=wdv, hv=hv: e.matmul(
                                PS[bk][0:n, :], lhsT=hv[:, j - j0, s * 128:s * 128 + n], rhs=wdv[:, kk, half * 512:(half + 1) * 512],
                                start=(j == j0), stop=(j == j0 + nch - 1)),
                              reads=[wdkey, ("hT", j - j0)], writes=["ps%d" % bk])
                wdone("dn%d" % pi)
            for s, t in enumerate(tiles):
                n = t["n"]
                for half in range(2):
                    bk = 2 * s + half
                    A("dve", lambda e, bk=bk, s=s, n=n, half=half: e.tensor_tensor(out=X[0:n, s, half * 512:(half + 1) * 512],
                                                                                 in0=X[0:n, s, half * 512:(half + 1) * 512],
                                                                                 in1=PS[bk][0:n, :], op=ALU.add),
                      reads=["ps%d" % bk, ("X", s)], writes=[("X", s)])
        A("pool", lambda e: e.memset(fz[:], 0.0), writes=ZK)
        for s, t in enumerate(tiles):
            n = t["n"]
            xs = X[0:n, s, :]
            A("act", lambda e, xs=xs, n=n: e.activation(out=xnb[0:n, :], in_=xs, func=AF.Square, accum_out=sm[0:n, 0:1]),
              reads=[("X", s)], writes=["xnb", "sm0"])
            A("act", lambda e, n=n: e.activation(out=sm[0:n, 1:2], in_=sm[0:n, 0:1], func=AF.Ln, scale=1.0 / D, bias=epsc[0:n, 0:1]),
              reads=["sm0", "epsc"], writes=["sm1"])
            A("act", lambda e, n=n: e.activation(out=sm[0:n, 1:2], in_=sm[0:n, 1:2], func=AF.Exp, scale=-0.5), reads=["sm1"], writes=["sm1"])
            yst = zst[0:n, (s % 2) * 1024:(s % 2) * 1024 + 1024]
            ykeys = [("zst", 2 * (s % 2)), ("zst", 2 * (s % 2) + 1)]
            A("dve", lambda e, xs=xs, n=n, yst=yst: e.scalar_tensor_tensor(out=yst, in0=xs, scalar=sm[0:n, 1:2], in1=gO[0:n, :],
                                                                          op0=ALU.mult, op1=ALU.mult),
              reads=[("X", s), "sm1", "gO"], writes=ykeys)
            A("sp", lambda e, t=t, yst=yst: e.dma_start(out=t["y"], in_=yst), reads=ykeys, dma=ykeys[0])
            if next_tiles is not None and s < len(next_tiles):
                nt_ = next_tiles[s]
                A("sp", lambda e, s=s, nt_=nt_: e.dma_start(out=X[0:nt_["n"], s, :], in_=nt_["x"]), writes=[("X", s)], dma=("X", s))

    nsup_per_seq = NBLK // NT
    if stop_after == "I":
        n_prompt_super = 0
        with_sample = False
    wpump()

    def prompt_tiles(sup):
        b = sup // nsup_per_seq
        i0 = (sup % nsup_per_seq) * NT
        tiles = []
        for s in range(NT):
            i = i0 + s
            r0, r1 = i * 128, (i + 1) * 128
            tiles.append(dict(n=128, x=x_p[b, r0:r1, :], cos=cosp[:, i, :], sin=sinp[:, i, :],
                              y=y_p[b, r0:r1, :], dk=p_dk[b, r0:r1, :], dv=p_dv[b, r0:r1, :],
                              sk=p_sk[b, r0:r1, :], sv=p_sv[b, r0:r1, :], blk=i))
        return tiles

    sample_tiles = [dict(n=SB * TS, x=x_s[:, :], cos=coss[:, :], sin=sins[:, :], y=y_s[:, :], dk=s_dk[:, :], dv=s_dv[:, :],
                         sk=s_sk[:, :], sv=s_sv[:, :], blk=None)]
    prefetch = stop_after is None
    for sup in range(n_prompt_super):
        wst["sup"] = sup
        b = sup // nsup_per_seq
        i0 = (sup % nsup_per_seq) * NT
        if i0 == 0:
            A("pool", lambda e: e.memset(CBp[:], 0.0), writes=["CBp"])
        if sup + 1 < n_prompt_super:
            nxt = prompt_tiles(sup + 1)
        elif with_sample:
            nxt = sample_tiles
        else:
            nxt = None
        super_tile(prompt_tiles(sup), sample=False, next_tiles=(nxt if prefetch else None),
                   x_loaded=(prefetch and sup > 0))
        if i0 + NT == NBLK:
            A("sp", lambda e, b=b: e.dma_start(out=p_cv[b], in_=CBp[:, :, 0, :]), reads=["CBp"], dma="CBp")
    if with_sample:
        wst["sup"] = n_prompt_super
        super_tile(sample_tiles, sample=True, x_loaded=(prefetch and n_prompt_super > 0))
        A("sp", lambda e: e.dma_start(out=s_cv[:, :, :, :], in_=CBs[:]), reads=["CBs"], dma="CBs_out")

    S.emit(st)
    st.close()
    return nc


_CACHE = {}


def _consts():
    bf = ml_dtypes.bfloat16
    r = np.arange(128)
    ident = np.eye(128, dtype=np.float32).astype(bf)
    tri = (-1.0 * (r[:, None] >= r[None, :])).astype(np.float32).astype(bf)
    onesn = (-np.ones((128, 128), np.float32)).astype(bf)
    nm1 = np.where(r[:, None] >= r[None, :], NEG, 0.0).astype(np.float32)
    nmsb = np.tile(nm1, (1, 4)).astype(bf)
    nmsb16 = np.tile(nm1[:16, :16], (1, 4)).astype(bf)
    nm2 = np.where((r[:, None] >= 64) & (r[None, :] < 64), NEG, 0.0).astype(np.float32)
    nmda = np.tile(nm2, (1, 4)).astype(bf)
    inv = (10000.0 ** (-np.arange(32, dtype=np.float32) * 2.0 / 64.0)).astype(np.float32)
    pos = np.arange(SEQ, dtype=np.float32)
    ang = (pos[:, None] * inv[None, :]).astype(np.float32)
    cosp = np.cos(ang).astype(np.float32).reshape(NBLK, 128, 32).transpose(1, 0, 2).copy()
    sinp = np.sin(ang).astype(np.float32).reshape(NBLK, 128, 32).transpose(1, 0, 2).copy()
    pos_s = (SEQ + np.arange(TS)).astype(np.float32)
    ang_s = (pos_s[:, None] * inv[None, :]).astype(np.float32)
    coss = np.tile(np.cos(ang_s).astype(np.float32), (SB, 1))
    sins = np.tile(np.sin(ang_s).astype(np.float32), (SB, 1))
    return dict(c_ident=ident, c_tri=tri, c_onesn=onesn, c_nmsb=nmsb, c_nmda=nmda, c_nmsb16=nmsb16,
                c_cosp=cosp, c_sinp=sinp, c_coss=coss, c_sins=sins)


def kernel(x_prompt, x_sample, cache_diff_k, cache_diff_v, cache_sb_k, cache_sb_v, state_conv,
           attn_norm_g, w_in, lambda_q1, lambda_k1, lambda_q2, lambda_k2, subln_g, w_branch_a,
           w_branch_b, w_out, ffn_norm_g, w_up, conv_w, conv_b, w_down, final_norm_g):
    f = lambda a: np.ascontiguousarray(np.asarray(a, dtype=np.float32))
    if "nc" not in _CACHE:
        _CACHE["nc"] = build_program()
    nc = _CACHE["nc"]
    cst = _consts()
    shared = dict(
        w_in=f(w_in[0]), w_bra=f(w_branch_a[0]), w_brb=f(w_branch_b[0]), w_out=f(w_out[0]), w_up=f(w_up[0]),
        w_down=f(w_down[0]), g_attn=f(np.asarray(attn_norm_g[0]).reshape(8, 128).T), g_ffn=f(np.asarray(ffn_norm_g[0]).reshape(8, 128).T), g_fin=f(final_norm_g),
        subln=f(subln_g[0]),
        lam4=f(np.concatenate([np.asarray(lambda_q1[0]), np.asarray(lambda_k1[0]), np.asarray(lambda_q2[0]), np.asarray(lambda_k2[0])])),
        conv_wt=f(np.asarray(conv_w[0]).reshape(3, 2 * NCH, 128).transpose(2, 1, 0)),
        conv_bt=f(np.asarray(conv_b[0]).reshape(2 * NCH, 128).transpose(1, 0)),
    )
    shared.update(cst)
    x_prompt = np.asarray(x_prompt); x_sample = np.asarray(x_sample)
    cdk = np.asarray(cache_diff_k); cdv = np.asarray(cache_diff_v); csk = np.asarray(cache_sb_k); csv = np.asarray(cache_sb_v)
    scv = np.asarray(state_conv)
    in_maps = []
    for c in range(NCORES):
        m = dict(shared)
        m["x_p"] = f(x_prompt[PB * c:PB * (c + 1)])
        m["x_s"] = f(x_sample[SB * c:SB * (c + 1)].reshape(SB * TS, D))
        m["c_dk"] = f(cdk[0, SB * c:SB * (c + 1)].reshape(SB, SEQ, 512))
        m["c_dv"] = f(cdv[0, SB * c:SB * (c + 1)].reshape(SB, SEQ, 512))
        m["c_sk"] = f(csk[0, SB * c:SB * (c + 1)].reshape(SB, SEQ, 512))
        m["c_sv"] = f(csv[0, SB * c:SB * (c + 1)].reshape(SB, SEQ, 512))
        m["st_cv"] = f(scv[0, SB * c:SB * (c + 1)].reshape(SB, 2, 2 * NCH, 128).transpose(3, 2, 0, 1))
        in_maps.append(m)
    res = run_bass_kernel_spmd(nc, in_maps, core_ids=list(range(NCORES)))
    R = res.results
    cat = lambda k: np.concatenate([np.asarray(r[k]) for r in R], axis=0)
    y_p = cat("y_p")
    y_s = cat("y_s").reshape(NCORES * SB, TS, D)
    p_dk = cat("p_dk").reshape(1, NCORES * PB, SEQ, 4, 2, 64)
    p_dv = cat("p_dv").reshape(1, NCORES * PB, SEQ, 4, 128)
    p_sk = cat("p_sk").reshape(1, NCORES * PB, SEQ, 8, 64)
    p_sv = cat("p_sv").reshape(1, NCORES * PB, SEQ, 8, 64)
    p_cv = cat("p_cv").transpose(0, 3, 2, 1).reshape(1, NCORES * PB, 2, 2 * DFF)
    s_dk = cat("s_dk").reshape(1, NCORES * SB, TS, 4, 2, 64)
    s_dv = cat("s_dv").reshape(1, NCORES * SB, TS, 4, 128)
    s_sk = cat("s_sk").reshape(1, NCORES * SB, TS, 8, 64)
    s_sv = cat("s_sv").reshape(1, NCORES * SB, TS, 8, 64)
    s_cv = np.concatenate([np.asarray(r["s_cv"]).transpose(2, 3, 1, 0).reshape(SB, 2, 2 * DFF) for r in R], axis=0)[None]
    outs = (y_p, y_s, p_dk, p_dv, p_sk, p_sv, p_cv, s_dk, s_dv, s_sk, s_sv, s_cv)
    return tuple(np.ascontiguousarray(o, dtype=np.float32) for o in outs)
```

```python
import numpy as np
import ml_dtypes
from contextlib import ExitStack
import concourse.bass as bass
import concourse.mybir as mybir
from concourse.bass_utils import run_bass_kernel_spmd

F32 = mybir.dt.float32
BF16 = mybir.dt.bfloat16
ALU = mybir.AluOpType
AF = mybir.ActivationFunctionType
AX = mybir.AxisListType

NCORES = 8
D = 1024
SEQ = 2048
NBLK = SEQ // 128
PB = 2
SB = 4
TS = 16
DFF = 2816
NCH = DFF // 128
NT = 4
TOK = NT * 128
NSLOT = 4
EPS = 1e-6
NEG = -30000.0


import types


def _snap(fn):
    if fn.__closure__ is None:
        return fn
    cells = []
    for c in fn.__closure__:
        try:
            cells.append(types.CellType(c.cell_contents))
        except ValueError:
            cells.append(c)
    return types.FunctionType(fn.__code__, fn.__globals__, fn.__name__, fn.__defaults__, tuple(cells))


class Sched:
    ENG = ("pe", "act", "dve", "pool", "sp")
    RAW_DIST = 10 ** 9

    def __init__(self, nc, same_engine_sync=True):
        self.nc = nc
        self.ops = []
        self.same = same_engine_sync
        self.last_w = {}
        self.readers = {}
        self.ecount = {}
        import os
        self.strict = os.environ.get("KSTRICT", "0") == "1"

    def add(self, eng, fn, reads=(), writes=(), dma=None):
        n = len(self.ops)
        deps = set()
        raw = set()
        for k in reads:
            if k in self.last_w:
                deps.add(self.last_w[k])
                raw.add(self.last_w[k])
            if isinstance(k, str) and k.startswith("ps"):
                for r in self.readers.get(k, ()):
                    if self.ops[r]["eng"] != eng:
                        deps.add(r)
        for k in writes:
            if k in self.last_w:
                deps.add(self.last_w[k])
            for r in self.readers.get(k, ()):
                deps.add(r)
        deps.discard(n)
        for k in reads:
            self.readers.setdefault(k, []).append(n)
        for k in writes:
            self.last_w[k] = n
            self.readers[k] = []
        eidx = self.ecount.get(eng, 0)
        self.ecount[eng] = eidx + 1
        keep = set()
        for d in deps:
            od = self.ops[d]
            if od["dma"] is None and od["eng"] == eng:
                if eng != "pe" and (self.strict or (d in raw and eidx - od["eidx"] <= self.RAW_DIST)):
                    keep.add(d)
            else:
                keep.add(d)
        self.ops.append(dict(eng=eng, fn=_snap(fn), deps=keep, dma=dma, sig=False, seq=None, eidx=eidx))
        return n

    def emit(self, stack):
        nc = self.nc
        ops = self.ops
        for n, op in enumerate(ops):
            for d in op["deps"]:
                od = ops[d]
                if od["dma"] is not None:
                    continue
                od["sig"] = True
        cnt = {}
        for op in ops:
            if op["dma"] is not None:
                key = ("dma", op["dma"])
                cnt[key] = cnt.get(key, 0) + 16
                op["seq"] = (key, cnt[key])
            elif op["sig"]:
                key = ("eng", op["eng"])
                cnt[key] = cnt.get(key, 0) + 1
                op["seq"] = (key, cnt[key])
        sems = {}
        for i, key in enumerate(cnt):
            sems[key] = stack.enter_context(nc.semaphore("sem%d" % i))
        per_eng = {e: [] for e in self.ENG}
        for n, op in enumerate(ops):
            per_eng[op["eng"]].append(n)
        block = stack.enter_context(nc.Block())
        same = self.same

        def run(engname, engobj):
            known = {}
            for n in per_eng[engname]:
                op = ops[n]
                need = {}
                for d in op["deps"]:
                    od = ops[d]
                    if od["seq"] is None:
                        continue
                    key, v = od["seq"]
                    if known.get(key, 0) >= v:
                        continue
                    need[key] = max(need.get(key, 0), v)
                for key, v in need.items():
                    engobj.wait_ge(sems[key], v)
                    known[key] = v
                ins = op["fn"](engobj)
                if op["seq"] is not None:
                    key, v = op["seq"]
                    ins.then_inc(sems[key], 16 if op["dma"] is not None else 1)
            if engname == "sp":
                for key, v in cnt.items():
                    if key[0] == "dma" and known.get(key, 0) < v:
                        engobj.wait_ge(sems[key], v)

        @block.tensor
        def _(e):
            run("pe", e)

        @block.scalar
        def _(e):
            run("act", e)

        @block.vector
        def _(e):
            run("dve", e)

        @block.gpsimd
        def _(e):
            run("pool", e)

        @block.sync
        def _(e):
            run("sp", e)


def build_program(n_prompt_super=PB * NBLK // NT, with_sample=True, stop_after=None):
    nc = bass.Bass("TRN2", target_bir_lowering=False)
    st = ExitStack()

    def din(name, shape, dt=F32):
        return nc.dram_tensor(name, list(shape), dt, kind="ExternalInput").ap()

    def dout(name, shape, dt=F32):
        return nc.dram_tensor(name, list(shape), dt, kind="ExternalOutput").ap()

    x_p = din("x_p", [PB, SEQ, D])
    x_s = din("x_s", [SB * TS, D])
    c_dk = din("c_dk", [SB, SEQ, 512])
    c_dv = din("c_dv", [SB, SEQ, 512])
    c_sk = din("c_sk", [SB, SEQ, 512])
    c_sv = din("c_sv", [SB, SEQ, 512])
    st_cv = din("st_cv", [128, 2 * NCH, SB, 2])
    w_in = din("w_in", [D, 5120])
    w_bra = din("w_bra", [512, D])
    w_brb = din("w_brb", [512, D])
    w_out = din("w_out", [D, D])
    w_up = din("w_up", [D, 2 * DFF])
    w_down = din("w_down", [DFF, D])
    g_attn = din("g_attn", [128, 8])
    g_ffn = din("g_ffn", [128, 8])
    g_fin = din("g_fin", [D])
    subln = din("subln", [128])
    lam4 = din("lam4", [4 * 64])
    conv_wt = din("conv_wt", [128, 2 * NCH, 3])
    conv_bt = din("conv_bt", [128, 2 * NCH])
    c_ident = din("c_ident", [128, 128], BF16)
    c_tri = din("c_tri", [128, 128], BF16)
    c_onesn = din("c_onesn", [128, 128], BF16)
    c_nmsb = din("c_nmsb", [128, 512], BF16)
    c_nmda = din("c_nmda", [128, 512], BF16)
    c_nmsb16 = din("c_nmsb16", [16, 64], BF16)
    c_cosp = din("c_cosp", [128, NBLK, 32])
    c_sinp = din("c_sinp", [128, NBLK, 32])
    c_coss = din("c_coss", [SB * TS, 32])
    c_sins = din("c_sins", [SB * TS, 32])

    y_p = dout("y_p", [PB, SEQ, D])
    y_s = dout("y_s", [SB * TS, D])
    p_dk = dout("p_dk", [PB, SEQ, 512])
    p_dv = dout("p_dv", [PB, SEQ, 512])
    p_sk = dout("p_sk", [PB, SEQ, 512])
    p_sv = dout("p_sv", [PB, SEQ, 512])
    p_cv = dout("p_cv", [PB, 128, 2 * NCH, 2])
    s_dk = dout("s_dk", [SB * TS, 512])
    s_dv = dout("s_dv", [SB * TS, 512])
    s_sk = dout("s_sk", [SB * TS, 512])
    s_sv = dout("s_sv", [SB * TS, 512])
    s_cv = dout("s_cv", [128, 2 * NCH, SB, 2])

    def sb(name, shape, dt):
        return st.enter_context(nc.sbuf_tensor(name, list(shape), dt))

    HCH = 12
    SCW = 520
    ident = sb("ident", [128, 128], BF16)
    tri = sb("tri", [128, 128], BF16)
    onesn = sb("onesn", [128, 128], BF16)
    nmsb = sb("nmsb", [128, 512], BF16)
    nmda = sb("nmda", [128, 512], BF16)
    nmsb16 = sb("nmsb16", [16, 64], BF16)
    one1 = sb("one1", [128, 1], F32)
    epsc = sb("epsc", [128, 1], F32)
    gAf = sb("gAf", [128, 8], F32)
    gFf = sb("gFf", [128, 8], F32)
    gO = sb("gO", [128, D], F32)
    gsub = sb("gsub", [128, 128], F32)
    l4 = sb("l4", [128, 4, 64], F32)
    lw = sb("lw", [128, 8], F32)
    cosp = sb("cosp", [128, NBLK, 32], F32)
    sinp = sb("sinp", [128, NBLK, 32], F32)
    coss = sb("coss", [SB * TS, 32], F32)
    sins = sb("sins", [SB * TS, 32], F32)
    cw = sb("cw", [128, 2 * NCH, 3], F32)
    cb = sb("cb", [128, 2 * NCH], F32)
    CBp = sb("CBp", [128, 2 * NCH, 1, 2], F32)
    CBs = sb("CBs", [128, 2 * NCH, SB, 2], F32)
    KaT = sb("KaT", [128, 4, SEQ], BF16)
    KbT = sb("KbT", [128, 4, SEQ], BF16)
    Va = sb("Va", [128, NBLK, 4, 130], BF16)
    Vb = sb("Vb", [128, NBLK, 512], BF16)
    KaTn = sb("KaTn", [128, 4, SB * TS], BF16)
    KbTn = sb("KbTn", [128, 4, SB * TS], BF16)
    X = sb("X", [128, NT, D], F32)
    xnT = sb("xnT", [128, 8, TOK], BF16)
    qaT = sb("qaT", [128, 4, TOK], BF16)
    qbT = sb("qbT", [128, 4, TOK], BF16)
    oaT = sb("oaT", [128, 4, TOK], BF16)
    obT = sb("obT", [128, 4, TOK], BF16)
    mT = sb("mT", [128, 8 * TOK], BF16)
    hT = sb("hT", [128, HCH * TOK], BF16)
    mTv = mT[:, :].rearrange("p (c t) -> p c t", c=8)
    Van = mT[0:TS, 0:SB * 4 * 130].rearrange("p (s h e) -> p s h e", s=SB, h=4)
    Vbn = hT[0:TS, 4096:4096 + SB * 512].rearrange("p (s f) -> p s f", s=SB)
    zst = sb("zst", [128, 2056], F32)
    fz = sb("fz", [128, 1], F32)
    ZCO = (0, 514, 1028, 1540)
    stg = dict(i=0)
    xnb = sb("xnb", [128, D], BF16)
    sm = sb("sm", [128, 32], F32)
    rA = sb("rA", [128, 8, 32], F32)
    rB = sb("rB", [128, 8, 32], F32)
    tbf2 = sb("tbf", [128, 2, 512], BF16)
    PA = sb("PA", [128, 4, 512], BF16)
    SP = sb("SP", [128, 4, 512], BF16)
    SPS = sb("SPS", [128, 2, 512], BF16)
    SCR = sb("SCR", [128, 4, SCW], F32)
    oa = sb("oa", [128, 4, 128], F32)
    oan = sb("oan", [128, 4, 128], BF16)
    WS = [sb("ws%d" % i, [128, 4096], BF16) for i in range(NSLOT)]

    PSALL = st.enter_context(nc.psum_tensor("psall", [128, 8 * 512], F32))
    PS = [PSALL[:, i * 512:(i + 1) * 512] for i in range(8)]
    PSB = [p.bitcast(BF16) for p in PS]

    def ps2(b0, nk, w):
        return PSALL[0:nk, b0 * 512:(b0 + 2) * 512].rearrange("p (b w) -> p b w", b=2)[:, :, 0:w]

    S = Sched(nc)
    A = S.add

    def ld(eng, dst, src, key):
        A(eng, lambda e: e.dma_start(out=dst, in_=src), writes=[key], dma=key)

    ld("sp", ident[:], c_ident[:, :], "ident")
    ld("sp", tri[:], c_tri[:, :], "tri")
    ld("sp", onesn[:], c_onesn[:, :], "onesn")
    ld("sp", nmsb[:], c_nmsb[:, :], "nmsb")
    ld("sp", nmda[:], c_nmda[:, :], "nmda")
    ld("sp", nmsb16[:], c_nmsb16[:, :], "nmsb16")
    ld("sp", gAf[:], g_attn[:, :], "gAf")
    ld("sp", gFf[:], g_ffn[:, :], "gFf")
    ld("sp", gO[:], g_fin.partition_broadcast(128), "gO")
    ld("sp", gsub[:], subln.partition_broadcast(128), "gsub")
    ld("sp", l4[:].rearrange("p a b -> p (a b)"), lam4.partition_broadcast(128), "l4")
    ld("sp", cosp[:], c_cosp[:, :, :], "cosp")
    ld("sp", sinp[:], c_sinp[:, :, :], "sinp")
    ld("sp", coss[:], c_coss[:, :], "coss")
    ld("sp", sins[:], c_sins[:, :], "sins")
    ld("sp", cw[:], conv_wt[:, :, :], "cw")
    ld("sp", cb[:], conv_bt[:, :], "cb")
    ld("sp", CBs[:], st_cv[:, :, :, :], "CBs")
    A("pool", lambda e: e.memset(one1[:], 1.0), writes=["one1"])
    A("pool", lambda e: e.memset(epsc[:], EPS), writes=["epsc"])
    A("pool", lambda e: e.memset(Va[:, :, :, 128:130], 1.0), writes=["Va"])
    A("dve", lambda e: e.tensor_scalar(out=gsub[:], in0=gsub[:], scalar1=0.8, scalar2=None, op0=ALU.mult),
      reads=["gsub"], writes=["gsub"])
    A("dve", lambda e: e.tensor_tensor(out=l4[:, 0, :], in0=l4[:, 0, :], in1=l4[:, 1, :], op=ALU.mult), reads=["l4"], writes=["l4"])
    A("dve", lambda e: e.tensor_tensor(out=l4[:, 2, :], in0=l4[:, 2, :], in1=l4[:, 3, :], op=ALU.mult), reads=["l4"], writes=["l4"])
    A("dve", lambda e: e.reduce_sum(out=lw[:, 0:1], in_=l4[:, 0, :], axis=AX.X), reads=["l4"], writes=["lw"])
    A("dve", lambda e: e.reduce_sum(out=lw[:, 1:2], in_=l4[:, 2, :], axis=AX.X), reads=["l4"], writes=["lw"])
    A("act", lambda e: e.activation(out=lw[:, 2:4], in_=lw[:, 0:2], func=AF.Exp), reads=["lw"], writes=["lw"])
    A("dve", lambda e: e.tensor_tensor(out=lw[:, 4:5], in0=lw[:, 3:4], in1=lw[:, 2:3], op=ALU.subtract), reads=["lw"], writes=["lw"])
    A("dve", lambda e: e.tensor_scalar(out=lw[:, 4:5], in0=lw[:, 4:5], scalar1=-0.2, scalar2=None, op0=ALU.add),
      reads=["lw"], writes=["lw"])
    neglam = lw[:, 4:5]

    def w_cols(w, c0, ncols):
        return w[:, c0:c0 + ncols].rearrange("(k p) n -> p k n", p=128)

    def w_rows(w, r0, nk):
        return w[r0:r0 + nk * 128, :].rearrange("(k p) n -> p k n", p=128)

    specs = []
    for g in (0, 1, 2, 4, 5, 3):
        specs.append(("in%d" % g, [(0, 8, 512, w_cols(w_in, g * 512, 512))]))
    specs.append(("wa", [(0, 4, 1024, w_rows(w_bra, 0, 4))]))
    specs.append(("ga0", [(0, 8, 512, w_cols(w_in, 3072, 512))]))
    specs.append(("ga1", [(0, 8, 512, w_cols(w_in, 3584, 512))]))
    specs.append(("wb", [(0, 4, 1024, w_rows(w_brb, 0, 4))]))
    specs.append(("gb0", [(0, 8, 512, w_cols(w_in, 4096, 512))]))
    specs.append(("gb1", [(0, 8, 512, w_cols(w_in, 4608, 512))]))
    specs.append(("wo0", [(0, 8, 512, w_cols(w_out, 0, 512))]))
    specs.append(("wo1", [(0, 8, 512, w_cols(w_out, 512, 512))]))
    for hf in range(2):
        for jp in (range(0, 6) if hf == 0 else range(6, 11)):
            specs.append(("up%d" % jp, [(0, 8, 256, w_cols(w_up, jp * 256, 256)),
                                       (2048, 8, 256, w_cols(w_up, DFF + jp * 256, 256))]))
        for pi in (range(0, 3) if hf == 0 else range(3, 6)):
            nkp = 4 if pi < 5 else 2
            specs.append(("dn%d" % pi, [(0, nkp, 1024, w_rows(w_down, pi * 512, nkp))]))
    NSPEC = len(specs)
    wsc = nc.dram_tensor("wsc", [NSPEC, 128, 4096], BF16).ap()
    plen = [sum(nk * ncols for (_, nk, ncols, _) in sp[1]) for sp in specs]
    spec_idx = {name: i for i, (name, _) in enumerate(specs)}
    n_super_total = n_prompt_super + (1 if with_sample else 0)
    wst = dict(free=list(range(NSLOT)), pending=0, loaded={}, sup=0)

    def wissue(gi, slot):
        key = "ws%d" % slot
        t = WS[slot]
        li = gi % NSPEC
        if gi < NSPEC or stop_after is not None:
            for (off, nk, ncols, ap) in specs[li][1]:
                dst = t[:, off:off + nk * ncols].rearrange("p (k n) -> p k n", k=nk)
                A("pool", lambda e, dst=dst, ap=ap: e.dma_start(out=dst, in_=ap), writes=[key], dma=key)
            if n_super_total > 1 and stop_after is None:
                A("sp", lambda e, t=t, li=li: e.dma_start(out=wsc[li, :, 0:plen[li]], in_=t[:, 0:plen[li]]),
                  reads=[key], writes=[("wsc", li)], dma=("wst", slot))
        else:
            A("pool", lambda e, t=t, li=li: e.dma_start(out=t[:, 0:plen[li]], in_=wsc[li, :, 0:plen[li]]),
              reads=[("wsc", li)], writes=[key], dma=key)

    def wpump():
        total = NSPEC * n_super_total
        if stop_after is not None:
            return
        while wst["free"] and wst["pending"] < total:
            slot = wst["free"].pop(0)
            gi = wst["pending"]
            wst["pending"] += 1
            wissue(gi, slot)
            wst["loaded"][gi] = slot

    def wget(name):
        gi = wst["sup"] * NSPEC + spec_idx[name]
        if gi not in wst["loaded"]:
            slot = wst["free"].pop(0)
            wissue(gi, slot)
            wst["loaded"][gi] = slot
        slot = wst["loaded"][gi]
        return WS[slot], "ws%d" % slot

    def wdone(name):
        gi = wst["sup"] * NSPEC + spec_idx[name]
        wst["free"].append(wst["loaded"].pop(gi))
        wpump()

    def rmsnorm_to_xnT(n, s, gf, gkey):
        xs = X[0:n, s, :]
        A("act", lambda e: e.activation(out=xnb[0:n, :], in_=xs, func=AF.Square, accum_out=sm[0:n, 0:1]),
          reads=[("X", s)], writes=["xnb", "sm0"])
        A("act", lambda e: e.activation(out=sm[0:n, 1:2], in_=sm[0:n, 0:1], func=AF.Ln, scale=1.0 / D, bias=epsc[0:n, 0:1]),
          reads=["sm0", "epsc"], writes=["sm1"])
        A("act", lambda e: e.activation(out=sm[0:n, 1:2], in_=sm[0:n, 1:2], func=AF.Exp, scale=-0.5), reads=["sm1"], writes=["sm1"])
        A("dve", lambda e: e.tensor_scalar(out=xnb[0:n, :], in0=xs, scalar1=sm[0:n, 1:2], scalar2=None, op0=ALU.mult),
          reads=[("X", s), "sm1"], writes=["xnb"])
        for c in range(8):
            A("pe", lambda e, c=c: e.transpose(out=PSB[7][:, c * 128:c * 128 + n], in_=xnb[0:n, c * 128:(c + 1) * 128],
                                               identity=ident[0:n, 0:n]),
              reads=["xnb", "ident"], writes=["ps7"])
        A("dve", lambda e: e.tensor_tensor(out=xnT[:, :, s * 128:s * 128 + n],
                                           in0=PSB[7][:, 0:1024].rearrange("p (c t) -> p c t", c=8)[:, :, 0:n],
                                           in1=gf[:, :].unsqueeze(2).to_broadcast([128, 8, n]), op=ALU.mult),
          reads=["ps7", gkey], writes=[("xnT", s)])

    def transpose4(src_bf, n, dst_fn, dst_keys, src_key):
        for h in range(4):
            A("pe", lambda e, h=h: e.transpose(out=PSB[6][:, h * 128:h * 128 + n], in_=src_bf[0:n, h * 128:(h + 1) * 128],
                                               identity=ident[0:n, 0:n]),
              reads=[src_key, "ident"], writes=["ps6"])
        A("act", lambda e: e.copy(out=dst_fn(), in_=PSB[6][:, 0:512].rearrange("p (c t) -> p c t", c=4)[:, :, 0:n]),
          reads=["ps6"], writes=dst_keys)

    def rope(zp, n, cos, sin, dst, rkeys, wkeys):
        zv = zp.rearrange("p (g t d) -> p g t d", g=8, t=2)
        dv = dst.rearrange("p (g t d) -> p g t d", g=8, t=2)
        cb_ = cos.unsqueeze(1).to_broadcast([n, 8, 32])
        sb_ = sin.unsqueeze(1).to_broadcast([n, 8, 32])
        x1 = zv[:, :, 0, :]
        x2 = zv[:, :, 1, :]
        A("dve", lambda e: e.tensor_tensor(out=rA[0:n], in0=x1, in1=cb_, op=ALU.mult), reads=rkeys, writes=["rA"])
        A("dve", lambda e: e.tensor_tensor(out=rB[0:n], in0=x2, in1=sb_, op=ALU.mult), reads=rkeys, writes=["rB"])
        A("dve", lambda e: e.tensor_tensor(out=dv[:, :, 0, :], in0=rA[0:n], in1=rB[0:n], op=ALU.subtract),
          reads=["rA", "rB"], writes=wkeys)
        A("dve", lambda e: e.tensor_tensor(out=rA[0:n], in0=x2, in1=cb_, op=ALU.mult), reads=rkeys, writes=["rA"])
        A("dve", lambda e: e.tensor_tensor(out=rB[0:n], in0=x1, in1=sb_, op=ALU.mult), reads=rkeys, writes=["rB"])
        A("dve", lambda e: e.tensor_tensor(out=dv[:, :, 1, :], in0=rA[0:n], in1=rB[0:n], op=ALU.add),
          reads=["rA", "rB"], writes=wkeys)

    def attention(nq, qcol, blocks):
        W4 = 4 * nq
        nb = len(blocks)
        qk = qcol // 128

        def d_qk(bi):
            blk = blocks[bi]
            nk = blk["nk"]
            stt = bi % 2
            for c in range(2):
                bk = 2 * stt + c
                bank = PS[bk]
                pkey = "ps%d" % bk
                first = True
                if blk["diag"] and nq == 128:
                    A("pe", lambda e, bank=bank, nk=nk: e.matmul(bank[0:nk, 0:W4], lhsT=ident[0:nk, 0:nk], rhs=nmda[0:nk, 0:W4],
                                                                start=True, stop=False, skip_group_check=True),
                      reads=["ident", "nmda"], writes=[pkey])
                    first = False
                for h in range(4):
                    A("pe", lambda e, bank=bank, nk=nk, h=h, c=c, blk=blk, fl=(first and h == 0):
                      e.matmul(bank[0:nk, h * nq:(h + 1) * nq], lhsT=blk["ka"](h)[64 * c:64 * c + 64, :],
                               rhs=qaT[64 * c:64 * c + 64, h, qcol:qcol + nq], start=fl, stop=(h == 3), skip_group_check=True),
                      reads=blk["kka"] + [("qaT", qk)], writes=[pkey])
            b0 = 2 * stt
            A("act", lambda e, nk=nk, b0=b0: e.activation(out=PA[0:nk, b0:b0 + 2, 0:W4], in_=ps2(b0, nk, W4),
                                                          func=AF.Exp, scale=0.125),
              reads=["ps%d" % b0, "ps%d" % (b0 + 1)], writes=[("PA", b0), ("PA", b0 + 1)])

        def d_pv(bi):
            blk = blocks[bi]
            nk = blk["nk"]
            stt = bi % 2
            for c in range(2):
                bk = 2 * stt + c
                for h in range(4):
                    ob = 4 + 2 * c + h // 2
                    A("pe", lambda e, nk=nk, h=h, bk=bk, ob=ob, blk=blk, bi=bi:
                      e.matmul(PS[ob][0:nq, (h % 2) * 129:(h % 2) * 129 + 129], lhsT=PA[0:nk, bk, h * nq:(h + 1) * nq],
                               rhs=blk["va"](h), start=(bi == 0 and h % 2 == 0), stop=(bi == nb - 1), skip_group_check=True),
                      reads=blk["kva"] + [("PA", bk)], writes=["ps%d" % ob])

        for bi in range(nb + 1):
            if bi < nb:
                d_qk(bi)
            if bi >= 1:
                d_pv(bi - 1)

        def epilogue():
            for c in range(2):
                for hp in range(2):
                    ob = 4 + 2 * c + hp
                    A("dve", lambda e, ob=ob, c=c, hp=hp: e.reciprocal(
                        out=sm[0:nq, 8 + 4 * c + 2 * hp:8 + 4 * c + 2 * hp + 2],
                        in_=PS[ob][0:nq, 0:258].rearrange("p (a b) -> p a b", a=2)[:, :, 128]),
                      reads=["ps%d" % ob], writes=["smrr"])
            A("dve", lambda e: e.tensor_scalar(out=sm[0:nq, 12:16], in0=sm[0:nq, 12:16], scalar1=neglam[0:nq, :], scalar2=None,
                                               op0=ALU.mult), reads=["smrr", "lw"], writes=["smrr"])
            for h in range(4):
                o0 = 4 + h // 2
                o1 = 6 + h // 2
                col = (h % 2) * 129
                A("dve", lambda e, h=h, o0=o0, col=col: e.tensor_scalar(out=oa[0:nq, h, :], in0=PS[o0][0:nq, col:col + 128],
                                                                      scalar1=sm[0:nq, 8 + h:9 + h], scalar2=None, op0=ALU.mult),
                  reads=["ps%d" % o0, "smrr"], writes=["oa"])
                A("dve", lambda e, h=h, o1=o1, col=col: e.scalar_tensor_tensor(out=oa[0:nq, h, :], in0=PS[o1][0:nq, col:col + 128],
                                                                             scalar=sm[0:nq, 12 + h:13 + h], in1=oa[0:nq, h, :],
                                                                             op0=ALU.mult, op1=ALU.add),
                  reads=["ps%d" % o1, "smrr", "oa"], writes=["oa"])
                A("dve", lambda e, h=h: e.scalar_tensor_tensor(out=oan[0:nq, h, :], in0=oa[0:nq, h, :], scalar=1.0, in1=oa[0:nq, h, :],
                                                              op0=ALU.mult, op1=ALU.mult, accum_out=sm[0:nq, 16 + h:17 + h]),
                  reads=["oa"], writes=["oan", "smss"])
            A("act", lambda e: e.activation(out=sm[0:nq, 20:24], in_=sm[0:nq, 16:20], func=AF.Ln, scale=1.0 / 128, bias=epsc[0:nq, 0:1]),
              reads=["smss", "epsc"], writes=["smrs"])
            A("act", lambda e: e.activation(out=sm[0:nq, 20:24], in_=sm[0:nq, 20:24], func=AF.Exp, scale=-0.5), reads=["smrs"], writes=["smrs"])
            for h in range(4):
                A("dve", lambda e, h=h: e.scalar_tensor_tensor(out=oan[0:nq, h, :], in0=oa[0:nq, h, :], scalar=sm[0:nq, 20 + h:21 + h],
                                                              in1=gsub[0:nq, :], op0=ALU.mult, op1=ALU.mult),
                  reads=["oa", "smrs", "gsub"], writes=["oan"])

        rb = list(reversed(blocks))

        def s_Q(bi):
            blk = rb[bi]
            nk = blk["nk"]
            stt = bi % 2
            for par in range(2):
                bk = 2 * stt + par
                bank = PS[bk]
                pkey = "ps%d" % bk
                first = True
                if blk["diag"]:
                    nm_t = nmsb if nq == 128 else nmsb16
                    A("pe", lambda e, bank=bank, nk=nk, nm_t=nm_t: e.matmul(bank[0:nk, 0:W4], lhsT=ident[0:nk, 0:nk],
                                                                           rhs=nm_t[0:nk, 0:W4], start=True, stop=False,
                                                                           skip_group_check=True),
                      reads=["ident", "nmsb", "nmsb16"], writes=[pkey])
                    first = False
                for p in range(4):
                    A("pe", lambda e, bank=bank, nk=nk, p=p, par=par, blk=blk, fl=(first and p == 0):
                      e.matmul(bank[0:nk, p * nq:(p + 1) * nq], lhsT=blk["kb"](p)[64 * par:64 * par + 64, :],
                               rhs=qbT[64 * par:64 * par + 64, p, qcol:qcol + nq], start=fl, stop=False, skip_group_check=True),
                      reads=blk["kkb"] + [("qbT", qk)], writes=[pkey])

        def s_E(bi):
            nk = rb[bi]["nk"]
            b0 = 2 * (bi % 2)
            A("act", lambda e, nk=nk, b0=b0: e.activation(out=SCR[0:nk, b0:b0 + 2, 0:W4], in_=ps2(b0, nk, W4), func=AF.Exp),
              reads=["ps%d" % b0, "ps%d" % (b0 + 1)], writes=[("SCR", b0), ("SCR", b0 + 1)])

        def s_L(bi):
            nk = rb[bi]["nk"]
            b0 = 2 * (bi % 2)
            A("act", lambda e, nk=nk, b0=b0: e.activation(out=SP[0:nk, b0:b0 + 2, 0:W4], in_=SCR[0:nk, b0:b0 + 2, 0:W4], func=AF.Ln,
                                                          bias=one1[0:nk, 0:1], scale=1.0),
              reads=[("SCR", b0), ("SCR", b0 + 1), "one1"], writes=[("SP", b0), ("SP", b0 + 1)])

        def s_C(bi):
            nk = rb[bi]["nk"]
            stt = bi % 2
            last = (bi == nb - 1)
            for par in range(2):
                bk = 2 * stt + par
                bank = PS[bk]
                pkey = "ps%d" % bk
                if bi > 0:
                    A("pe", lambda e, bank=bank, nk=nk, par=par: e.matmul(bank[0:nk, 0:W4], lhsT=onesn[:, 0:nk],
                                                                         rhs=SPS[:, par, 0:W4], start=False, stop=False,
                                                                         skip_group_check=True),
                      reads=["onesn", ("SPS", par)], writes=[pkey])
            for par in range(2):
                bk = 2 * stt + par
                bank = PS[bk]
                pkey = "ps%d" % bk
                A("pe", lambda e, bank=bank, nk=nk, bk=bk: e.matmul(bank[0:nk, 0:W4], lhsT=tri[0:nk, 0:nk],
                                                                   rhs=SP[0:nk, bk, 0:W4], start=False, stop=True,
                                                                   skip_group_check=True),
                  reads=["tri", ("SP", bk)], writes=[pkey])
            b0 = 2 * stt
            if not last:
                spk = [("SP", b0), ("SP", b0 + 1)]
                spsk = [("SPS", 0), ("SPS", 1)]
                if bi == 0:
                    if nk < 128:
                        A("pool", lambda e: e.memset(SPS[:, :, 0:W4], 0.0), writes=spsk)
                    A("dve", lambda e, nk=nk, b0=b0: e.tensor_copy(out=SPS[0:nk, :, 0:W4], in_=SP[0:nk, b0:b0 + 2, 0:W4]),
                      reads=spk, writes=spsk)
                else:
                    A("dve", lambda e, nk=nk, b0=b0: e.tensor_tensor(out=SPS[0:nk, :, 0:W4], in0=SPS[0:nk, :, 0:W4],
                                                                    in1=SP[0:nk, b0:b0 + 2, 0:W4], op=ALU.add),
                      reads=spk + spsk, writes=spsk)

        def s_F(bi):
            nk = rb[bi]["nk"]
            b0 = 2 * (bi % 2)
            A("act", lambda e, nk=nk, b0=b0: e.activation(out=PA[0:nk, b0:b0 + 2, 0:W4], in_=ps2(b0, nk, W4), func=AF.Exp),
              reads=["ps%d" % b0, "ps%d" % (b0 + 1)], writes=[("PA", b0), ("PA", b0 + 1)])

        def s_V(bi):
            blk = rb[bi]
            nk = blk["nk"]
            stt = bi % 2
            last = (bi == nb - 1)
            for par in range(2):
                bk = 2 * stt + par
                for p in range(4):
                    h = 2 * p + par
                    A("pe", lambda e, nk=nk, h=h, p=p, par=par, bk=bk, blk=blk, bi=bi, last=last:
                      e.matmul(PS[5][64 * par:64 * par + 64, p * nq:(p + 1) * nq], lhsT=blk["vb"](h),
                               rhs=PA[0:nk, bk, p * nq:(p + 1) * nq], start=(bi == 0 and p == 0), stop=last, skip_group_check=True),
                      reads=blk["kvb"] + [("PA", bk)], writes=["ps5"])

        s_Q(0)
        s_E(0)
        s_L(0)
        for b in range(nb):
            if b + 1 < nb:
                s_Q(b + 1)
            s_C(b)
            if b >= 1:
                s_V(b - 1)
            if b + 1 < nb:
                s_E(b + 1)
            s_F(b)
            if b + 1 < nb:
                s_L(b + 1)
            if b == 0:
                epilogue()
        s_V(nb - 1)
        A("dve", lambda e: e.tensor_copy(out=obT[:, :, qcol:qcol + nq],
                                         in_=PS[5][:, 0:W4].rearrange("p (c t) -> p c t", c=4)),
          reads=["ps5"], writes=[("obT", qk)])
        for h in range(4):
            A("pe", lambda e, h=h: e.transpose(out=PSB[4][:, h * 128:h * 128 + nq], in_=oan[0:nq, h, :],
                                               identity=ident[0:nq, 0:nq]),
              reads=["oan", "ident", "oa"], writes=["ps4"])
        A("dve", lambda e: e.tensor_copy(out=oaT[:, :, qcol:qcol + nq],
                                         in_=PSB[4][:, 0:512].rearrange("p (c t) -> p c t", c=4)[:, :, 0:nq]),
          reads=["ps4"], writes=[("oaT", qk)])

    def super_tile(tiles, sample, next_tiles=None, x_loaded=False):
        ntile = len(tiles)
        ntok = sum(t["n"] for t in tiles)
        for s, t in enumerate(tiles):
            n = t["n"]
            if not x_loaded:
                A("sp", lambda e, s=s, t=t, n=n: e.dma_start(out=X[0:n, s, :], in_=t["x"]), writes=[("X", s)], dma=("X", s))
            rmsnorm_to_xnT(n, s, gAf, "gAf")
        if stop_after == "A1":
            return
        xkeys = [("xnT", s) for s in range(ntile)]
        cskeys = ["coss", "sins"] if sample else ["cosp", "sinp"]
        pend = []
        pab = dict(i=0)

        def flush_pend():
            while pend:
                pend.pop(0)()

        for g in (0, 1, 2, 4, 5, 3):
            wt, wkey = wget("in%d" % g)
            wv = wt[:, :].rearrange("p (k n) -> p k n", k=8)
            if g == 3:
                for p in range(4):
                    bank = PS[p % 4]
                    pkey = "ps%d" % (p % 4)
                    for k in range(8):
                        A("pe", lambda e, bank=bank, k=k, p=p: e.matmul(bank[:, 0:ntok], lhsT=wv[:, k, p * 128:(p + 1) * 128],
                                                                       rhs=xnT[:, k, 0:ntok], start=(k == 0), stop=(k == 7)),
                          reads=[wkey] + xkeys, writes=[pkey])
                    flush_pend()
                    A("act", lambda e, bank=bank, p=p: e.activation(out=qbT[:, p, 0:ntok], in_=bank[:, 0:ntok], func=AF.Copy, scale=0.125),
                      reads=[pkey], writes=[("qbT", s) for s in range(ntile)])
                wdone("in3")
                continue
            for s, t in enumerate(tiles):
                n = t["n"]
                bk = pab["i"] % 6
                pab["i"] += 1
                bank = PS[bk]
                pkey = "ps%d" % bk
                for k in range(8):
                    A("pe", lambda e, bank=bank, k=k, s=s, n=n: e.matmul(bank[0:n, :], lhsT=xnT[:, k, s * 128:s * 128 + n],
                                                                        rhs=wv[:, k, :], start=(k == 0), stop=(k == 7)),
                      reads=[wkey, ("xnT", s)], writes=[pkey])
                flush_pend()
                tbf = tbf2[:, s % 2, :]
                tbk = ("tbf", s % 2)
                if g == 0:
                    rope(bank[0:n, :], n, t["cos"], t["sin"], tbf[0:n, :], [pkey] + cskeys, [tbk])
                    pend.append(lambda s=s, n=n, tbf=tbf, tbk=tbk: transpose4(
                        tbf, n, lambda s=s, n=n: qaT[:, :, s * 128:s * 128 + n], [("qaT", s)], tbk))
                else:
                    zr = stg["i"] % 4
                    stg["i"] += 1
                    zs = zst[:, zr * 512:(zr + 1) * 512]
                    zk = ("zst", zr)
                if g == 0:
                    pass
                elif g == 1:
                    rope(bank[0:n, :], n, t["cos"], t["sin"], zs[0:n, :], [pkey] + cskeys, [zk])
                    A("sp", lambda e, t=t, n=n, zs=zs: e.dma_start(out=t["dk"], in_=zs[0:n, :]), reads=[zk], dma=zk)
                    A("act", lambda e, n=n, tbf=tbf, zs=zs: e.copy(out=tbf[0:n, :], in_=zs[0:n, :]), reads=[zk], writes=[tbk])
                    if sample:
                        pend.append(lambda n=n, tbf=tbf, tbk=tbk: transpose4(tbf, n, lambda: KaTn[:, :, 0:n], ["KaTn"], tbk))
                    else:
                        pend.append(lambda n=n, t=t, tbf=tbf, tbk=tbk: transpose4(
                            tbf, n, lambda t=t: KaT[:, :, t["blk"] * 128:(t["blk"] + 1) * 128], [("ka", t["blk"])], tbk))
                elif g == 2:
                    A("act", lambda e, bank=bank, n=n, zs=zs: e.copy(out=zs[0:n, :], in_=bank[0:n, :]), reads=[pkey], writes=[zk])
                    A("sp", lambda e, t=t, n=n, zs=zs: e.dma_start(out=t["dv"], in_=zs[0:n, :]), reads=[zk], dma=zk)
                    if sample:
                        for j in range(SB):
                            A("pool", lambda e, j=j, zs=zs: e.dma_start(out=Van[:, j, :, 0:128],
                                                                        in_=zs[j * TS:(j + 1) * TS, :].rearrange("p (h d) -> p h d", h=4)),
                              reads=[zk], writes=["Van"], dma="Van")
                    else:
                        A("dve", lambda e, t=t, zs=zs: e.tensor_copy(out=Va[:, t["blk"], :, 0:128],
                                                                     in_=zs[:, :].rearrange("p (h d) -> p h d", h=4)),
                          reads=[zk], writes=[("va", t["blk"])])
                elif g == 4:
                    A("act", lambda e, bank=bank, n=n, zs=zs: e.copy(out=zs[0:n, :], in_=bank[0:n, :]), reads=[pkey], writes=[zk])
                    A("sp", lambda e, t=t, n=n, zs=zs: e.dma_start(out=t["sk"], in_=zs[0:n, :]), reads=[zk], dma=zk)
                    A("dve", lambda e, n=n, tbf=tbf, zs=zs: e.tensor_copy(out=tbf[0:n, :], in_=zs[0:n, :]), reads=[zk], writes=[tbk])
                    if sample:
                        pend.append(lambda n=n, tbf=tbf, tbk=tbk: transpose4(tbf, n, lambda: KbTn[:, :, 0:n], ["KbTn"], tbk))
                    else:
                        pend.append(lambda n=n, t=t, tbf=tbf, tbk=tbk: transpose4(
                            tbf, n, lambda t=t: KbT[:, :, t["blk"] * 128:(t["blk"] + 1) * 128], [("kb", t["blk"])], tbk))
                elif g == 5:
                    A("act", lambda e, bank=bank, n=n, zs=zs: e.copy(out=zs[0:n, :], in_=bank[0:n, :]), reads=[pkey], writes=[zk])
                    A("sp", lambda e, t=t, n=n, zs=zs: e.dma_start(out=t["sv"], in_=zs[0:n, :]), reads=[zk], dma=zk)
                    if sample:
                        for j in range(SB):
                            A("pool", lambda e, j=j, zs=zs: e.dma_start(out=Vbn[:, j, :], in_=zs[j * TS:(j + 1) * TS, :]),
                              reads=[zk], writes=["Vbn"], dma="Vbn")
                    else:
                        A("dve", lambda e, t=t, zs=zs: e.tensor_copy(out=Vb[:, t["blk"], :], in_=zs[:, :]),
                          reads=[zk], writes=[("vb", t["blk"])])
            wdone("in%d" % g)
        flush_pend()
        if stop_after == "A":
            return
        if not sample:
            for s, t in enumerate(tiles):
                blocks = []
                for kb in range(t["blk"] + 1):
                    blocks.append(dict(
                        nk=128, diag=(kb == t["blk"]),
                        ka=lambda h, kb=kb: KaT[:, h, kb * 128:(kb + 1) * 128],
                        va=lambda h, kb=kb: Va[:, kb, h, 0:129],
                        kb=lambda p, kb=kb: KbT[:, p, kb * 128:(kb + 1) * 128],
                        vb=lambda h, kb=kb: Vb[:, kb, h * 64:(h + 1) * 64],
                        kka=[("ka", kb)], kva=[("va", kb)], kkb=[("kb", kb)], kvb=[("vb", kb)]))
                attention(128, s * 128, blocks)
        else:
            A("pool", lambda e: e.memset(Van[:, :, :, 128:130], 1.0), reads=["mT"], writes=["Van"])
            ksts = [(hT[:, 0:4096].rearrange("p (b f) -> p b f", b=8), ["hT"], "kst0"),
                    (X[:, 1:3, :].rearrange("p a b -> p (a b)").bitcast(BF16).rearrange("p (b f) -> p b f", b=8),
                     [("X", 1), ("X", 2)], "kst1")]
            chunks = [(c_dk, KaT, 0), (c_dk, KaT, 1), (c_sk, KbT, 0), (c_sk, KbT, 1)]

            def kload(j, ci):
                csrc, _, half = chunks[ci]
                kst, kkeys, kdma = ksts[ci % 2]
                A("pool", lambda e: e.dma_start(
                    out=kst, in_=csrc[j, half * 1024:(half + 1) * 1024, :].rearrange("(b p) f -> p b f", p=128)),
                  writes=kkeys, dma=kdma)

            kload(0, 0)
            kload(0, 1)
            for j in range(SB):
                allkv = [("kv", kb) for kb in range(NBLK)]
                for h in range(4):
                    A("pool", lambda e, j=j, h=h: e.dma_start(out=Va[:, :, h, 0:128],
                                                              in_=c_dv[j, :, h * 128:(h + 1) * 128].rearrange("(b p) d -> p b d", p=128)),
                      writes=[("va", kb) for kb in range(NBLK)], dma="cva")
                A("pool", lambda e, j=j: e.dma_start(out=Vb[:, :, :], in_=c_sv[j].rearrange("(b p) f -> p b f", p=128)),
                  writes=[("vb", kb) for kb in range(NBLK)], dma="cvb")
                for ci in range(4):
                    if True:
                        csrc, KT, half = chunks[ci]
                        kst, kkeys, kdma = ksts[ci % 2]
                        for b2 in range(4):
                            pb = 6 + (b2 % 2)
                            for bb in range(2):
                                for h in range(4):
                                    A("pe", lambda e, pb=pb, bb=bb, h=h, b2=b2, kst=kst: e.transpose(
                                        out=PSB[pb][:, (bb * 4 + h) * 128:(bb * 4 + h + 1) * 128],
                                        in_=kst[:, b2 * 2 + bb, h * 128:(h + 1) * 128], identity=ident[:, :]),
                                      reads=kkeys + ["ident"], writes=["ps%d" % pb])
                            k0 = (half * 8 + b2 * 2) * 128
                            if b2 % 2 == 0:
                                A("act", lambda e, pb=pb, KT=KT, k0=k0: e.copy(
                                    out=KT[:, :, k0:k0 + 256].rearrange("p h (b t) -> p b h t", b=2),
                                    in_=PSB[pb][:, 0:1024].rearrange("p (b h t) -> p b h t", b=2, h=4)),
                                  reads=["ps%d" % pb], writes=[(("ka" if KT is KaT else "kb"), kb) for kb in range(NBLK)])
                            else:
                                A("dve", lambda e, pb=pb, KT=KT, k0=k0: e.tensor_copy(
                                    out=KT[:, :, k0:k0 + 256].rearrange("p h (b t) -> p b h t", b=2),
                                    in_=PSB[pb][:, 0:1024].rearrange("p (b h t) -> p b h t", b=2, h=4)),
                                  reads=["ps%d" % pb], writes=[(("ka" if KT is KaT else "kb"), kb) for kb in range(NBLK)])
                        if ci + 2 < 4:
                            kload(j, ci + 2)
                        elif j + 1 < SB:
                            kload(j + 1, ci - 2)
                blocks = []
                for kb in range(NBLK):
                    blocks.append(dict(
                        nk=128, diag=False,
                        ka=lambda h, kb=kb: KaT[:, h, kb * 128:(kb + 1) * 128],
                        va=lambda h, kb=kb: Va[:, kb, h, 0:129],
                        kb=lambda p, kb=kb: KbT[:, p, kb * 128:(kb + 1) * 128],
                        vb=lambda h, kb=kb: Vb[:, kb, h * 64:(h + 1) * 64],
                        kka=[("ka", kb)], kva=[("va", kb)], kkb=[("kb", kb)], kvb=[("vb", kb)]))
                blocks.append(dict(
                    nk=TS, diag=True,
                    ka=lambda h, j=j: KaTn[:, h, j * TS:(j + 1) * TS],
                    va=lambda h, j=j: Van[:, j, h, 0:129],
                    kb=lambda p, j=j: KbTn[:, p, j * TS:(j + 1) * TS],
                    vb=lambda h, j=j: Vbn[:, j, h * 64:(h + 1) * 64],
                    kka=["KaTn"], kva=["Van"], kkb=["KbTn"], kvb=["Vbn"]))
                attention(TS, j * TS, blocks)
        if stop_after == "B":
            return
        okeys = [("oaT", s) for s in range(ntile)] + [("obT", s) for s in range(ntile)]
        for (bname, gname, srcT, first_pass) in (("wa", "ga", oaT, True), ("wb", "gb", obT, False)):
            wbr, wbrkey = wget(bname)
            wbrv = wbr[:, :].rearrange("p (k n) -> p k n", k=4)
            for gp in range(2):
                wg, wgkey = wget("%s%d" % (gname, gp))
                wgv = wg[:, :].rearrange("p (k n) -> p k n", k=8)
                for cc in range(4):
                    c = gp * 4 + cc
                    b0 = 2 * (c % 4)
                    for k in range(4):
                        A("pe", lambda e, b0=b0, k=k, c=c, wbrv=wbrv, srcT=srcT: e.matmul(
                            PS[b0][:, 0:ntok], lhsT=wbrv[:, k, c * 128:(c + 1) * 128], rhs=srcT[:, k, 0:ntok],
                            start=(k == 0), stop=(k == 3)),
                          reads=[wbrkey] + okeys, writes=["ps%d" % b0])
                    for k in range(8):
                        A("pe", lambda e, b0=b0, k=k, cc=cc, wgv=wgv: e.matmul(
                            PS[b0 + 1][:, 0:ntok], lhsT=wgv[:, k, cc * 128:(cc + 1) * 128], rhs=xnT[:, k, 0:ntok],
                            start=(k == 0), stop=(k == 7)),
                          reads=[wgkey] + xkeys, writes=["ps%d" % (b0 + 1)])
                    si = c % 2
                    A("act", lambda e, b0=b0, si=si: e.activation(out=SCR[:, si, 0:ntok], in_=PS[b0 + 1][:, 0:ntok], func=AF.Sigmoid),
                      reads=["ps%d" % (b0 + 1)], writes=[("SCR", si)])
                    if first_pass:
                        A("dve", lambda e, b0=b0, si=si, c=c: e.tensor_tensor(out=mTv[:, c, 0:ntok], in0=SCR[:, si, 0:ntok],
                                                                             in1=PS[b0][:, 0:ntok], op=ALU.mult),
                          reads=[("SCR", si), "ps%d" % b0], writes=["mT"])
                    else:
                        A("dve", lambda e, b0=b0, si=si: e.tensor_tensor(out=SCR[:, 2 + si, 0:ntok], in0=SCR[:, si, 0:ntok],
                                                                        in1=PS[b0][:, 0:ntok], op=ALU.mult),
                          reads=[("SCR", si), "ps%d" % b0], writes=[("SCR", 2 + si)])
                        A("dve", lambda e, si=si, c=c: e.tensor_tensor(out=mTv[:, c, 0:ntok], in0=mTv[:, c, 0:ntok],
                                                                      in1=SCR[:, 2 + si, 0:ntok], op=ALU.add),
                          reads=[("SCR", 2 + si), "mT"], writes=["mT"])
                wdone("%s%d" % (gname, gp))
            wdone(bname)
        for half in range(2):
            wo, wokey = wget("wo%d" % half)
            wov = wo[:, :].rearrange("p (k n) -> p k n", k=8)
            for s, t in enumerate(tiles):
                n = t["n"]
                bk = (2 * s + half) % 8
                for k in range(8):
                    A("pe", lambda e, bk=bk, k=k, s=s, n=n, wov=wov: e.matmul(PS[bk][0:n, :], lhsT=mTv[:, k, s * 128:s * 128 + n],
                                                                             rhs=wov[:, k, :], start=(k == 0), stop=(k == 7)),
                      reads=[wokey, "mT"], writes=["ps%d" % bk])
                A("dve", lambda e, bk=bk, s=s, n=n, half=half: e.tensor_tensor(out=X[0:n, s, half * 512:(half + 1) * 512],
                                                                             in0=X[0:n, s, half * 512:(half + 1) * 512],
                                                                             in1=PS[bk][0:n, :], op=ALU.add),
                  reads=["ps%d" % bk, ("X", s)], writes=[("X", s)])
            wdone("wo%d" % half)
        if stop_after == "C":
            return
        for s, t in enumerate(tiles):
            rmsnorm_to_xnT(t["n"], s, gFf, "gFf")
        CB = CBs if sample else CBp
        cbkey = "CBs" if sample else "CBp"
        nseq = SB if sample else 1
        L = ntok // nseq

        ZK = [("zst", k) for k in range(4)] + [("zc", k) for k in range(4)]

        def cbuf(cs, i, width):
            if cs == 0:
                return SCR[:, i, 0:width], ("SCR", i)
            return zst[:, ZCO[i]:ZCO[i] + width], ("zc", i)

        A("pool", lambda e: e.memset(fz[:], 0.0), writes=ZK)

        tails = []
        for hf in range(2):
            j0 = 0 if hf == 0 else HCH
            nch = HCH if hf == 0 else NCH - HCH
            hv = hT[:, 0:nch * ntok].rearrange("p (c t) -> p c t", c=nch)
            for jp in (range(0, 6) if hf == 0 else range(6, 11)):
                wu, wukey = wget("up%d" % jp)
                wug = wu[:, 0:2048].rearrange("p (k n) -> p k n", k=8)
                wuv = wu[:, 2048:4096].rearrange("p (k n) -> p k n", k=8)
                for jj in range(2):
                    j = 2 * jp + jj
                    bg = 2 * (j % 4)
                    for (bnk, wv_) in ((bg, wug), (bg + 1, wuv)):
                        for k in range(8):
                            A("pe", lambda e, bnk=bnk, wv_=wv_, k=k, jj=jj: e.matmul(
                                PS[bnk][:, 0:ntok], lhsT=wv_[:, k, jj * 128:(jj + 1) * 128], rhs=xnT[:, k, 0:ntok],
                                start=(k == 0), stop=(k == 7)),
                              reads=[wukey] + xkeys, writes=["ps%d" % bnk])
                    cs = j % 2
                    for (bnk, ui, ai, ch) in ((bg, 0, 2, j), (bg + 1, 1, 3, NCH + j)):
                        ub, uk = cbuf(cs, ui, nseq * (L + 2))
                        ab, ak = cbuf(cs, ai, ntok)
                        uv = ub.rearrange("p (a b) -> p a b", a=nseq)
                        av = ab.rearrange("p (a b) -> p a b", a=nseq)
                        uck = ("uc",) + tuple(uk)
                        A("pool", lambda e, uv=uv, ch=ch: e.tensor_copy(out=uv[:, :, 0:2], in_=CB[:, ch, :, :]),
                          reads=[cbkey], writes=[uck])
                        A("act", lambda e, uv=uv, bnk=bnk: e.copy(out=uv[:, :, 2:L + 2],
                                                                  in_=PS[bnk][:, 0:ntok].rearrange("p (a b) -> p a b", a=nseq)),
                          reads=["ps%d" % bnk], writes=[uk])
                        A("pool", lambda e, uv=uv, ch=ch: e.tensor_copy(out=CB[:, ch, :, :], in_=uv[:, :, L:L + 2]),
                          reads=[uk], writes=[cbkey])
                        A("act", lambda e, av=av, bnk=bnk, ch=ch: e.activation(
                            out=av, in_=PS[bnk][:, 0:ntok].rearrange("p (a b) -> p a b", a=nseq), func=AF.Identity,
                            scale=cw[:, ch, 2:3], bias=cb[:, ch:ch + 1]),
                          reads=["ps%d" % bnk, "cw", "cb"], writes=[ak])
                        A("dve", lambda e, uv=uv, av=av, ch=ch: e.scalar_tensor_tensor(
                            out=av, in0=uv[:, :, 1:L + 1], scalar=cw[:, ch, 1:2], in1=av, op0=ALU.mult, op1=ALU.add),
                          reads=[uk, uck, "cw", ak], writes=[ak])
                        A("dve", lambda e, uv=uv, av=av, ch=ch: e.scalar_tensor_tensor(out=av, in0=uv[:, :, 0:L], scalar=cw[:, ch, 0:1],
                                                                                      in1=av, op0=ALU.mult, op1=ALU.add),
                          reads=[uk, uck, "cw", ak], writes=[ak])
                    gb_, gk = cbuf(cs, 2, ntok)
                    vb_, vk = cbuf(cs, 3, ntok)

                    def tail(gb_=gb_, gk=gk, vb_=vb_, vk=vk, j=j, j0=j0, hv=hv):
                        A("act", lambda e: e.activation(out=gb_, in_=gb_, func=AF.Silu), reads=[gk], writes=[gk])
                        A("dve", lambda e: e.tensor_tensor(out=hv[:, j - j0, :], in0=gb_, in1=vb_, op=ALU.mult),
                          reads=[gk, vk], writes=[("hT", j - j0)])
                    while tails:
                        tails.pop(0)()
                    tails.append(tail)
                wdone("up%d" % jp)
            while tails:
                tails.pop(0)()
            for pi in (range(0, 3) if hf == 0 else range(3, 6)):
                nkp = 4 if pi < 5 else 2
                wd, wdkey = wget("dn%d" % pi)
                wdv = wd[:, 0:nkp * 1024].rearrange("p (k n) -> p k n", k=nkp)
                for s, t in enumerate(tiles):
                    n = t["n"]
                    for half in range(2):
                        bk = 2 * s + half
                        for kk in range(nkp):
                            j = pi * 4 + kk
                            A("pe", lambda e, bk=bk, kk=kk, j=j, j0=j0, nch=nch, s=s, n=n, half=half, wdv=wdv, hv=hv: e.matmul(
                                PS[bk][0:n, :], lhsT=hv[:, j - j0, s * 128:s * 128 + n], rhs=wdv[:, kk, half * 512:(half + 1) * 512],
                                start=(j == j0), stop=(j == j0 + nch - 1)),
                              reads=[wdkey, ("hT", j - j0)], writes=["ps%d" % bk])
                wdone("dn%d" % pi)
            for s, t in enumerate(tiles):
                n = t["n"]
                for half in range(2):
                    bk = 2 * s + half
                    A("dve", lambda e, bk=bk, s=s, n=n, half=half: e.tensor_tensor(out=X[0:n, s, half * 512:(half + 1) * 512],
                                                                                 in0=X[0:n, s, half * 512:(half + 1) * 512],
                                                                                 in1=PS[bk][0:n, :], op=ALU.add),
                      reads=["ps%d" % bk, ("X", s)], writes=[("X", s)])
        A("pool", lambda e: e.memset(fz[:], 0.0), writes=ZK)
        for s, t in enumerate(tiles):
            n = t["n"]
            xs = X[0:n, s, :]
            A("act", lambda e, xs=xs, n=n: e.activation(out=xnb[0:n, :], in_=xs, func=AF.Square, accum_out=sm[0:n, 0:1]),
              reads=[("X", s)], writes=["xnb", "sm0"])
            A("act", lambda e, n=n: e.activation(out=sm[0:n, 1:2], in_=sm[0:n, 0:1], func=AF.Ln, scale=1.0 / D, bias=epsc[0:n, 0:1]),
              reads=["sm0", "epsc"], writes=["sm1"])
            A("act", lambda e, n=n: e.activation(out=sm[0:n, 1:2], in_=sm[0:n, 1:2], func=AF.Exp, scale=-0.5), reads=["sm1"], writes=["sm1"])
            yst = zst[0:n, (s % 2) * 1024:(s % 2) * 1024 + 1024]
            ykeys = [("zst", 2 * (s % 2)), ("zst", 2 * (s % 2) + 1)]
            A("dve", lambda e, xs=xs, n=n, yst=yst: e.scalar_tensor_tensor(out=yst, in0=xs, scalar=sm[0:n, 1:2], in1=gO[0:n, :],
                                                                          op0=ALU.mult, op1=ALU.mult),
              reads=[("X", s), "sm1", "gO"], writes=ykeys)
            A("sp", lambda e, t=t, yst=yst: e.dma_start(out=t["y"], in_=yst), reads=ykeys, dma=ykeys[0])
            if next_tiles is not None and s < len(next_tiles):
                nt_ = next_tiles[s]
                A("sp", lambda e, s=s, nt_=nt_: e.dma_start(out=X[0:nt_["n"], s, :], in_=nt_["x"]), writes=[("X", s)], dma=("X", s))

    nsup_per_seq = NBLK // NT
    if stop_after == "I":
        n_prompt_super = 0
        with_sample = False
    wpump()

    def prompt_tiles(sup):
        b = sup // nsup_per_seq
        i0 = (sup % nsup_per_seq) * NT
        tiles = []
        for s in range(NT):
            i = i0 + s
            r0, r1 = i * 128, (i + 1) * 128
            tiles.append(dict(n=128, x=x_p[b, r0:r1, :], cos=cosp[:, i, :], sin=sinp[:, i, :],
                              y=y_p[b, r0:r1, :], dk=p_dk[b, r0:r1, :], dv=p_dv[b, r0:r1, :],
                              sk=p_sk[b, r0:r1, :], sv=p_sv[b, r0:r1, :], blk=i))
        return tiles

    sample_tiles = [dict(n=SB * TS, x=x_s[:, :], cos=coss[:, :], sin=sins[:, :], y=y_s[:, :], dk=s_dk[:, :], dv=s_dv[:, :],
                         sk=s_sk[:, :], sv=s_sv[:, :], blk=None)]
    prefetch = stop_after is None
    for sup in range(n_prompt_super):
        wst["sup"] = sup
        b = sup // nsup_per_seq
        i0 = (sup % nsup_per_seq) * NT
        if i0 == 0:
            A("pool", lambda e: e.memset(CBp[:], 0.0), writes=["CBp"])
        if sup + 1 < n_prompt_super:
            nxt = prompt_tiles(sup + 1)
        elif with_sample:
            nxt = sample_tiles
        else:
            nxt = None
        super_tile(prompt_tiles(sup), sample=False, next_tiles=(nxt if prefetch else None),
                   x_loaded=(prefetch and sup > 0))
        if i0 + NT == NBLK:
            A("sp", lambda e, b=b: e.dma_start(out=p_cv[b], in_=CBp[:, :, 0, :]), reads=["CBp"], dma="CBp")
    if with_sample:
        wst["sup"] = n_prompt_super
        super_tile(sample_tiles, sample=True, x_loaded=(prefetch and n_prompt_super > 0))
        A("sp", lambda e: e.dma_start(out=s_cv[:, :, :, :], in_=CBs[:]), reads=["CBs"], dma="CBs_out")

    S.emit(st)
    st.close()
    return nc


_CACHE = {}


def _consts():
    bf = ml_dtypes.bfloat16
    r = np.arange(128)
    ident = np.eye(128, dtype=np.float32).astype(bf)
    tri = (-1.0 * (r[:, None] >= r[None, :])).astype(np.float32).astype(bf)
    onesn = (-np.ones((128, 128), np.float32)).astype(bf)
    nm1 = np.where(r[:, None] >= r[None, :], NEG, 0.0).astype(np.float32)
    nmsb = np.tile(nm1, (1, 4)).astype(bf)
    nmsb16 = np.tile(nm1[:16, :16], (1, 4)).astype(bf)
    nm2 = np.where((r[:, None] >= 64) & (r[None, :] < 64), NEG, 0.0).astype(np.float32)
    nmda = np.tile(nm2, (1, 4)).astype(bf)
    inv = (10000.0 ** (-np.arange(32, dtype=np.float32) * 2.0 / 64.0)).astype(np.float32)
    pos = np.arange(SEQ, dtype=np.float32)
    ang = (pos[:, None] * inv[None, :]).astype(np.float32)
    cosp = np.cos(ang).astype(np.float32).reshape(NBLK, 128, 32).transpose(1, 0, 2).copy()
    sinp = np.sin(ang).astype(np.float32).reshape(NBLK, 128, 32).transpose(1, 0, 2).copy()
    pos_s = (SEQ + np.arange(TS)).astype(np.float32)
    ang_s = (pos_s[:, None] * inv[None, :]).astype(np.float32)
    coss = np.tile(np.cos(ang_s).astype(np.float32), (SB, 1))
    sins = np.tile(np.sin(ang_s).astype(np.float32), (SB, 1))
    return dict(c_ident=ident, c_tri=tri, c_onesn=onesn, c_nmsb=nmsb, c_nmda=nmda, c_nmsb16=nmsb16,
                c_cosp=cosp, c_sinp=sinp, c_coss=coss, c_sins=sins)


def kernel(x_prompt, x_sample, cache_diff_k, cache_diff_v, cache_sb_k, cache_sb_v, state_conv,
           attn_norm_g, w_in, lambda_q1, lambda_k1, lambda_q2, lambda_k2, subln_g, w_branch_a,
           w_branch_b, w_out, ffn_norm_g, w_up, conv_w, conv_b, w_down, final_norm_g):
    f = lambda a: np.ascontiguousarray(np.asarray(a, dtype=np.float32))
    if "nc" not in _CACHE:
        _CACHE["nc"] = build_program()
    nc = _CACHE["nc"]
    cst = _consts()
    shared = dict(
        w_in=f(w_in[0]), w_bra=f(w_branch_a[0]), w_brb=f(w_branch_b[0]), w_out=f(w_out[0]), w_up=f(w_up[0]),
        w_down=f(w_down[0]), g_attn=f(np.asarray(attn_norm_g[0]).reshape(8, 128).T), g_ffn=f(np.asarray(ffn_norm_g[0]).reshape(8, 128).T), g_fin=f(final_norm_g),
        subln=f(subln_g[0]),
        lam4=f(np.concatenate([np.asarray(lambda_q1[0]), np.asarray(lambda_k1[0]), np.asarray(lambda_q2[0]), np.asarray(lambda_k2[0])])),
        conv_wt=f(np.asarray(conv_w[0]).reshape(3, 2 * NCH, 128).transpose(2, 1, 0)),
        conv_bt=f(np.asarray(conv_b[0]).reshape(2 * NCH, 128).transpose(1, 0)),
    )
    shared.update(cst)
    x_prompt = np.asarray(x_prompt); x_sample = np.asarray(x_sample)
    cdk = np.asarray(cache_diff_k); cdv = np.asarray(cache_diff_v); csk = np.asarray(cache_sb_k); csv = np.asarray(cache_sb_v)
    scv = np.asarray(state_conv)
    in_maps = []
    for c in range(NCORES):
        m = dict(shared)
        m["x_p"] = f(x_prompt[PB * c:PB * (c + 1)])
        m["x_s"] = f(x_sample[SB * c:SB * (c + 1)].reshape(SB * TS, D))
        m["c_dk"] = f(cdk[0, SB * c:SB * (c + 1)].reshape(SB, SEQ, 512))
        m["c_dv"] = f(cdv[0, SB * c:SB * (c + 1)].reshape(SB, SEQ, 512))
        m["c_sk"] = f(csk[0, SB * c:SB * (c + 1)].reshape(SB, SEQ, 512))
        m["c_sv"] = f(csv[0, SB * c:SB * (c + 1)].reshape(SB, SEQ, 512))
        m["st_cv"] = f(scv[0, SB * c:SB * (c + 1)].reshape(SB, 2, 2 * NCH, 128).transpose(3, 2, 0, 1))
        in_maps.append(m)
    res = run_bass_kernel_spmd(nc, in_maps, core_ids=list(range(NCORES)))
    R = res.results
    cat = lambda k: np.concatenate([np.asarray(r[k]) for r in R], axis=0)
    y_p = cat("y_p")
    y_s = cat("y_s").reshape(NCORES * SB, TS, D)
    p_dk = cat("p_dk").reshape(1, NCORES * PB, SEQ, 4, 2, 64)
    p_dv = cat("p_dv").reshape(1, NCORES * PB, SEQ, 4, 128)
    p_sk = cat("p_sk").reshape(1, NCORES * PB, SEQ, 8, 64)
    p_sv = cat("p_sv").reshape(1, NCORES * PB, SEQ, 8, 64)
    p_cv = cat("p_cv").transpose(0, 3, 2, 1).reshape(1, NCORES * PB, 2, 2 * DFF)
    s_dk = cat("s_dk").reshape(1, NCORES * SB, TS, 4, 2, 64)
    s_dv = cat("s_dv").reshape(1, NCORES * SB, TS, 4, 128)
    s_sk = cat("s_sk").reshape(1, NCORES * SB, TS, 8, 64)
    s_sv = cat("s_sv").reshape(1, NCORES * SB, TS, 8, 64)
    s_cv = np.concatenate([np.asarray(r["s_cv"]).transpose(2, 3, 1, 0).reshape(SB, 2, 2 * DFF) for r in R], axis=0)[None]
    outs = (y_p, y_s, p_dk, p_dv, p_sk, p_sv, p_cv, s_dk, s_dv, s_sk, s_sv, s_cv)
    return tuple(np.ascontiguousarray(o, dtype=np.float32) for o in outs)
```
